# Optimizing a Trainium2 kernel written in Bass

```python
import math
import jax, jax.numpy as jnp
from jax import lax
import numpy as np

D_MODEL = 2048
BATCH = 2
SEQ = 16384
DEPTH = 2

MEM_LEN = 256
HALF = D_MODEL // 2

POOL_WINDOWS = (2, 4, 8, 16)
POOL_GROUPS = len(POOL_WINDOWS)
POOL_GDIM = HALF // POOL_GROUPS

GDN_HEADS = 8
GDN_DK = HALF // GDN_HEADS
GDN_DV = HALF // GDN_HEADS
GDN_CONV = 4
GDN_CHUNK = 64
NORM_EPS = 1e-6

RWKV_HEAD = 64
RWKV_HEADS = HALF // RWKV_HEAD
RWKV_DECAY_LORA = 64
RWKV_A_LORA = 64
RWKV_GATE_LORA = 160
RWKV_LNX_EPS = 64e-5
RWKV_SPLITS = (HALF, HALF, HALF, RWKV_DECAY_LORA, RWKV_A_LORA, RWKV_GATE_LORA)
RWKV_IN = sum(RWKV_SPLITS)

FOX_HEADS = 8
FOX_HD = HALF // FOX_HEADS
FOX_BLOCK = 128

EVEN_SPLITS = (HALF, HALF, HALF, HALF, HALF, GDN_HEADS, GDN_HEADS)
EVEN_IN = sum(EVEN_SPLITS)
ODD_SPLITS = (RWKV_IN, HALF, HALF, HALF, FOX_HEADS)
ODD_IN = sum(ODD_SPLITS)

XA_HEADS = 4
XA_HD = D_MODEL // XA_HEADS

D_FF = 5632
FFN_CONV = 3

N_EVEN = (DEPTH + 1) // 2
N_ODD = DEPTH // 2
DEEPNORM_ALPHA = float((2 * DEPTH) ** 0.25)
DEEPNORM_BETA = float((8 * DEPTH) ** -0.25)
LN_EPS = 1e-5

kernel_name = 'hybrid_pool_gdn_rwkv7_fox_trunk'


def split_sizes(t, sizes):
    offs = np.cumsum(sizes)[:-1].tolist()
    return jnp.split(t, offs, axis=-1)


def layer_norm(x, w, b):
    xf = x.astype(jnp.float32)
    mu = jnp.mean(xf, -1, keepdims=True)
    var = jnp.mean(jnp.square(xf - mu), -1, keepdims=True)
    return ((xf - mu) * lax.rsqrt(var + LN_EPS) * w + b).astype(x.dtype)


def l2norm(t):
    return t * lax.rsqrt(jnp.sum(t * t, -1, keepdims=True) + NORM_EPS)


def causal_dwconv(x, w):
    K, S = w.shape[0], x.shape[1]
    xp = jnp.pad(x, ((0, 0), (K - 1, 0), (0, 0)))
    return sum(xp[:, i:i + S] * w[i] for i in range(K))


def pool_mixer(u, pool_w, pool_scale):
    f32 = jnp.float32
    uf = u.astype(f32)
    S = uf.shape[1]
    cs = jnp.pad(jnp.cumsum(uf, axis=1), ((0, 0), (1, 0), (0, 0)))
    upper = cs[:, 1:]
    pos = jnp.arange(1, S + 1, dtype=f32)
    outs = []
    for g, win in enumerate(POOL_WINDOWS):
        sl = slice(g * POOL_GDIM, (g + 1) * POOL_GDIM)
        lower = jnp.pad(cs[:, :S + 1 - win, sl], ((0, 0), (win - 1, 0), (0, 0)))
        mean = (upper[..., sl] - lower) / jnp.minimum(pos, win)[None, :, None]
        outs.append(jnp.einsum('bsc,cd->bsd', mean - uf[..., sl], pool_w[g].astype(f32)))
    return (jnp.concatenate(outs, -1) * pool_scale).astype(u.dtype)


def gated_deltanet(q, k, v, z, a_logit, b_logit, conv_w, a_log, dt_bias, norm_w):
    f32 = jnp.float32
    B_, S, _ = q.shape
    H, dk, dv, C = GDN_HEADS, GDN_DK, GDN_DV, GDN_CHUNK
    N = S // C
    qkv = jax.nn.silu(causal_dwconv(jnp.concatenate([q, k, v], -1), conv_w).astype(f32))
    q, k, v = jnp.split(qkv, 3, axis=-1)
    q = l2norm(q.reshape(B_, S, H, dk)) * dk ** -0.5
    k = l2norm(k.reshape(B_, S, H, dk))
    v = v.reshape(B_, S, H, dv)
    g = -jnp.exp(a_log.astype(f32)) * jax.nn.softplus(a_logit.astype(f32) + dt_bias)
    beta = jax.nn.sigmoid(b_logit.astype(f32))
    to_chunks = lambda t: t.reshape(B_, N, C, H, -1).transpose(0, 3, 1, 2, 4)
    qc, kc, vc = to_chunks(q), to_chunks(k), to_chunks(v)
    gc = jnp.cumsum(to_chunks(g[..., None])[..., 0], axis=-1)
    bc = to_chunks(beta[..., None])
    causal = jnp.tril(jnp.ones((C, C), dtype=bool))
    strict = jnp.tril(jnp.ones((C, C), dtype=bool), -1)
    diff = gc[..., :, None] - gc[..., None, :]
    decay = jnp.where(causal, jnp.exp(jnp.where(causal, diff, 0.0)), 0.0)
    kb = kc * bc
    L = jnp.where(strict, jnp.einsum('bhnid,bhnjd->bhnij', kb, kc) * decay, 0.0)
    eye = jnp.eye(C, dtype=f32)
    T = lax.linalg.triangular_solve(eye + L, jnp.broadcast_to(eye, L.shape),
                                    left_side=True, lower=True, unit_diagonal=True)
    u = T @ (vc * bc)
    w = T @ (kb * jnp.exp(gc)[..., None])
    intra = jnp.where(causal, jnp.einsum('bhnid,bhnjd->bhnij', qc, kc) * decay, 0.0)
    g_last = gc[..., -1]
    q_dec = qc * jnp.exp(gc)[..., None]
    k_dec = kc * jnp.exp(g_last[..., None] - gc)[..., None]

    def chunk_step(state, xs):
        q_i, k_i, u_i, w_i, a_i, gl_i = xs
        v_new = u_i - w_i @ state
        o_i = q_i @ state + a_i @ v_new
        state = state * jnp.exp(gl_i)[..., None, None] + jnp.swapaxes(k_i, -1, -2) @ v_new
        return state, o_i

    xs = tuple(jnp.moveaxis(t, 2, 0) for t in (q_dec, k_dec, u, w, intra, g_last))
    _, o = lax.scan(chunk_step, jnp.zeros((B_, H, dk, dv), f32), xs)
    o = o.transpose(1, 0, 3, 2, 4).reshape(B_, S, H, dv)
    o = o * lax.rsqrt(jnp.mean(o * o, -1, keepdims=True) + NORM_EPS) * norm_w
    o = o * jax.nn.silu(z.astype(f32).reshape(B_, S, H, dv))
    return o.reshape(B_, S, HALF).astype(z.dtype)


def rwkv7_time_mix(c_blk, mu, w0, w2, a0, a2, g2, k_k, k_a, r_k, lnx_w, lnx_b):
    f32 = jnp.float32
    c_blk = c_blk.astype(f32)
    prev = jnp.pad(c_blk, ((0, 0), (1, 0), (0, 0)))[:, :-1]
    c_blk = c_blk + (prev - c_blk) * mu
    r, k, v, wl, al, gl = split_sizes(c_blk, RWKV_SPLITS)
    w_log = -jax.nn.softplus(-(w0 + jnp.tanh(wl) @ w2)) - 0.5
    decay = jnp.exp(-jnp.exp(w_log))
    a = jax.nn.sigmoid(a0 + al @ a2)
    g = jax.nn.sigmoid(gl) @ g2
    B_, S, _ = r.shape
    heads = lambda t: t.reshape(B_, S, RWKV_HEADS, RWKV_HEAD)
    kk = l2norm(heads(k * k_k))
    k = k * (1.0 + (a - 1.0) * k_a)
    r_h, k_h, v_h, w_h, a_h = heads(r), heads(k), heads(v), heads(decay), heads(a)

    def step(state, xs):
        r_t, w_t, k_t, v_t, aa_t, bb_t = xs
        sa = jnp.einsum('bhvk,bhk->bhv', state, aa_t)
        state = (state * w_t[:, :, None, :] + sa[..., None] * bb_t[:, :, None, :]
                 + v_t[..., None] * k_t[:, :, None, :])
        return state, jnp.einsum('bhvk,bhk->bhv', state, r_t)

    xs = tuple(jnp.moveaxis(t, 1, 0) for t in (r_h, w_h, k_h, v_h, -kk, kk * a_h))
    _, y = lax.scan(step, jnp.zeros((B_, RWKV_HEADS, RWKV_HEAD, RWKV_HEAD), f32), xs)
    y = jnp.moveaxis(y, 0, 1)
    ym = jnp.mean(y, -1, keepdims=True)
    yv = jnp.mean(jnp.square(y - ym), -1, keepdims=True)
    y = ((y - ym) * lax.rsqrt(yv + RWKV_LNX_EPS)).reshape(B_, S, HALF) * lnx_w + lnx_b
    bonus = jnp.sum(r_h * k_h * r_k, -1, keepdims=True) * v_h
    return ((y + bonus.reshape(B_, S, HALF)) * g)


def forgetting_attention(q, k, v, f_logit, b_f):
    f32 = jnp.float32
    B_, S, _ = q.shape
    H, hd, FB = FOX_HEADS, FOX_HD, FOX_BLOCK
    nb = S // FB
    heads = lambda t: t.astype(f32).reshape(B_, S, H, hd).transpose(0, 2, 1, 3)
    qh, kh, vh = heads(q) * hd ** -0.5, heads(k), heads(v)
    c = jnp.cumsum(jax.nn.log_sigmoid(f_logit.astype(f32) + b_f), axis=1).transpose(0, 2, 1)
    kpos = jnp.arange(S)

    def block(args):
        q_i, c_i, i = args
        s = jnp.einsum('bhqd,bhkd->bhqk', q_i, kh) + (c_i[..., :, None] - c[..., None, :])
        qpos = i * FB + jnp.arange(FB)
        s = jnp.where(kpos[None, :] <= qpos[:, None], s, -jnp.inf)
        return jnp.einsum('bhqk,bhkd->bhqd', jax.nn.softmax(s, axis=-1), vh)

    qb = qh.reshape(B_, H, nb, FB, hd).transpose(2, 0, 1, 3, 4)
    cb = c.reshape(B_, H, nb, FB).transpose(2, 0, 1, 3)
    o = lax.map(block, (qb, cb, jnp.arange(nb)))
    return o.transpose(1, 0, 3, 2, 4).reshape(B_, S, HALF)


def even_mixer(x, w_in, pool_w, pool_scale, conv_w, a_log, dt_bias, norm_w, w_out):
    h = x @ w_in
    u_pool, q, k, v, z, a_logit, b_logit = split_sizes(h, EVEN_SPLITS)
    y_a = pool_mixer(u_pool, pool_w, pool_scale)
    y_b = gated_deltanet(q, k, v, z, a_logit, b_logit, conv_w, a_log, dt_bias, norm_w)
    return jnp.concatenate([y_a, y_b.astype(y_a.dtype)], -1) @ w_out


def odd_mixer(x, w_in, mu, w0, w2, a0, a2, g2, k_k, k_a, r_k, lnx_w, lnx_b, b_f, w_out):
    h = x @ w_in
    c_blk, fq, fk, fv, f_logit = split_sizes(h, ODD_SPLITS)
    y_c = rwkv7_time_mix(c_blk, mu, w0, w2, a0, a2, g2, k_k, k_a, r_k, lnx_w, lnx_b)
    y_d = forgetting_attention(fq, fk, fv, f_logit, b_f)
    return jnp.concatenate([y_c, y_d], -1).astype(x.dtype) @ w_out


def memory_cross_attention(x, mem, w_q, w_kv, w_o):
    B_, S, _ = x.shape
    M = mem.shape[1]
    q = (x @ w_q).reshape(B_, S, XA_HEADS, XA_HD).astype(jnp.float32)
    k, v = jnp.split((mem @ w_kv).astype(jnp.float32), 2, axis=-1)
    k = k.reshape(B_, M, XA_HEADS, XA_HD)
    v = v.reshape(B_, M, XA_HEADS, XA_HD)
    p = jax.nn.softmax(jnp.einsum('bshd,bmhd->bhsm', q, k) * XA_HD ** -0.5, axis=-1)
    o = jnp.einsum('bhsm,bmhd->bshd', p, v).reshape(B_, S, D_MODEL).astype(x.dtype)
    return o @ w_o


def conv_glu_ffn(x, w_up, conv_w, w_down):
    h = causal_dwconv(x @ w_up, conv_w)
    gate, up = jnp.split(h, 2, axis=-1)
    return (jax.nn.silu(gate) * up) @ w_down


def setup_inputs(seed: int = 0) -> dict:
    key = jax.random.key(seed)
    ks = iter(jax.random.split(key, 64))
    f32 = jnp.float32
    nrm = lambda shape, scale: jax.random.normal(next(ks), shape, f32) * scale
    unif = lambda shape, lo, hi: jax.random.uniform(next(ks), shape, f32, lo, hi)
    gain = lambda shape: 1.0 + nrm(shape, 0.02)
    NE, NO, L = N_EVEN, N_ODD, DEPTH
    dt = jnp.exp(unif((NE, GDN_HEADS), math.log(1e-3), math.log(1e-1)))
    return {
        'x': nrm((BATCH, SEQ, D_MODEL), 1.0),
        'mem': nrm((BATCH, MEM_LEN, D_MODEL), 1.0),
        'ev_w_in': nrm((NE, D_MODEL, EVEN_IN), D_MODEL ** -0.5),
        'pool_w': nrm((NE, POOL_GROUPS, POOL_GDIM, POOL_GDIM), POOL_GDIM ** -0.5),
        'pool_scale': gain((NE, HALF)),
        'gdn_conv_w': nrm((NE, GDN_CONV, 3 * HALF), GDN_CONV ** -0.5),
        'gdn_a_log': jnp.log(unif((NE, GDN_HEADS), 1.0, 16.0)),
        'gdn_dt_bias': dt + jnp.log(-jnp.expm1(-dt)),
        'gdn_norm_w': gain((NE, GDN_DV)),
        'ev_w_out': nrm((NE, D_MODEL, D_MODEL), D_MODEL ** -0.5 * DEEPNORM_BETA),
        'od_w_in': nrm((NO, D_MODEL, ODD_IN), D_MODEL ** -0.5),
        'rwkv_mu': unif((NO, RWKV_IN), 0.0, 1.0),
        'rwkv_w0': unif((NO, HALF), -6.0, -1.0),
        'rwkv_w2': nrm((NO, RWKV_DECAY_LORA, HALF), RWKV_DECAY_LORA ** -0.5),
        'rwkv_a0': nrm((NO, HALF), 0.1),
        'rwkv_a2': nrm((NO, RWKV_A_LORA, HALF), RWKV_A_LORA ** -0.5),
        'rwkv_g2': nrm((NO, RWKV_GATE_LORA, HALF), RWKV_GATE_LORA ** -0.5),
        'rwkv_k_k': 0.85 + nrm((NO, HALF), 0.05),
        'rwkv_k_a': 1.0 + nrm((NO, HALF), 0.05),
        'rwkv_r_k': nrm((NO, RWKV_HEADS, RWKV_HEAD), 0.1),
        'rwkv_lnx_w': gain((NO, HALF)),
        'rwkv_lnx_b': nrm((NO, HALF), 0.02),
        'fox_b_f': unif((NO, FOX_HEADS), 1.0, 6.0),
        'od_w_out': nrm((NO, D_MODEL, D_MODEL), D_MODEL ** -0.5 * DEEPNORM_BETA),
        'ln_mix_w': gain((L, D_MODEL)),
        'ln_mix_b': nrm((L, D_MODEL), 0.02),
        'xa_w_q': nrm((L, D_MODEL, D_MODEL), D_MODEL ** -0.5),
        'xa_w_kv': nrm((L, D_MODEL, 2 * D_MODEL), D_MODEL ** -0.5),
        'xa_w_o': nrm((L, D_MODEL, D_MODEL), D_MODEL ** -0.5 * DEEPNORM_BETA),
        'ln_xa_w': gain((L, D_MODEL)),
        'ln_xa_b': nrm((L, D_MODEL), 0.02),
        'ffn_w_up': nrm((L, D_MODEL, 2 * D_FF), D_MODEL ** -0.5),
        'ffn_conv_w': nrm((L, FFN_CONV, 2 * D_FF), FFN_CONV ** -0.5),
        'ffn_w_down': nrm((L, D_FF, D_MODEL), D_FF ** -0.5 * DEEPNORM_BETA),
        'ln_ffn_w': gain((L, D_MODEL)),
        'ln_ffn_b': nrm((L, D_MODEL), 0.02),
    }


def reference(x, mem, ev_w_in, pool_w, pool_scale, gdn_conv_w, gdn_a_log, gdn_dt_bias, gdn_norm_w, ev_w_out,
              od_w_in, rwkv_mu, rwkv_w0, rwkv_w2, rwkv_a0, rwkv_a2, rwkv_g2, rwkv_k_k, rwkv_k_a, rwkv_r_k,
              rwkv_lnx_w, rwkv_lnx_b, fox_b_f, od_w_out,
              ln_mix_w, ln_mix_b, xa_w_q, xa_w_kv, xa_w_o, ln_xa_w, ln_xa_b,
              ffn_w_up, ffn_conv_w, ffn_w_down, ln_ffn_w, ln_ffn_b):
    for i in range(DEPTH):
        j = i // 2
        if i % 2 == 0:
            y = even_mixer(x, ev_w_in[j], pool_w[j], pool_scale[j], gdn_conv_w[j], gdn_a_log[j],
                           gdn_dt_bias[j], gdn_norm_w[j], ev_w_out[j])
        else:
            y = odd_mixer(x, od_w_in[j], rwkv_mu[j], rwkv_w0[j], rwkv_w2[j], rwkv_a0[j], rwkv_a2[j],
                          rwkv_g2[j], rwkv_k_k[j], rwkv_k_a[j], rwkv_r_k[j], rwkv_lnx_w[j], rwkv_lnx_b[j],
                          fox_b_f[j], od_w_out[j])
        x = layer_norm(DEEPNORM_ALPHA * x + y, ln_mix_w[i], ln_mix_b[i])
        x = layer_norm(DEEPNORM_ALPHA * x + memory_cross_attention(x, mem, xa_w_q[i], xa_w_kv[i], xa_w_o[i]),
                       ln_xa_w[i], ln_xa_b[i])
        x = layer_norm(DEEPNORM_ALPHA * x + conv_glu_ffn(x, ffn_w_up[i], ffn_conv_w[i], ffn_w_down[i]),
                       ln_ffn_w[i], ln_ffn_b[i])
    return x
```

```python
import contextlib
import numpy as np
import concourse.bass as bass
import concourse.mybir as mybir
from concourse.bass_utils import run_bass_kernel_spmd

F32 = mybir.dt.float32
BF16 = mybir.dt.bfloat16
AF = mybir.ActivationFunctionType
ALU = mybir.AluOpType
AX = mybir.AxisListType

NCORES = 8
D = 2048
NCH = 16
ALPHA = float(4 ** 0.25)
LN_EPS = 1e-5


class Tl:
    __slots__ = ("h", "writers", "readers", "semkey", "name")

    def __init__(self, h, name):
        self.h = h
        self.name = name
        self.writers = []
        self.readers = []
        self.semkey = None

    def __getitem__(self, idx):
        return self.h[idx]


class StopBuild(Exception):
    pass


STOP_AT = [None]


def ckpt(name):
    if STOP_AT[0] == name:
        raise StopBuild(name)


class KB:
    def __init__(self):
        nc = bass.Bass("TRN2", target_bir_lowering=False)
        self.nc = nc
        self.eng = {"pe": nc.tensor, "dve": nc.vector, "act": nc.scalar, "pool": nc.gpsimd, "sp": nc.sync}
        self.semh = {}
        self.cnt = {}
        self.seen = {e: {} for e in self.eng}
        for e in self.eng:
            self.semh[e] = nc.alloc_semaphore("sem_" + e)
            self.cnt[e] = 0
        self.nps = 0
        self.ps_tiles = []
        self.uid = 0
        self.out_tokens = []

    def sb(self, name, shape, dtype=F32):
        if getattr(self, "scope", None) is not None:
            return Tl(self.scope.enter_context(self.nc.sbuf_tensor(name, list(shape), dtype)), name)
        return Tl(self.nc.alloc_sbuf_tensor(name, list(shape), dtype), name)

    def psum_banks(self, n=8):
        self.ps_tiles = [Tl(self.nc.alloc_psum_tensor("psb%d" % i, [128, 512], F32), "psb%d" % i) for i in range(n)]
        return self.ps_tiles

    def ps(self):
        t = self.ps_tiles[self.nps % len(self.ps_tiles)]
        self.nps += 1
        return t

    def dram(self, name, shape, dtype=F32, kind="Internal"):
        t = Tl(self.nc.dram_tensor(name, list(shape), dtype, kind=kind).ap(), name)
        return t

    def _dmasem(self, t):
        if t.semkey is None:
            t.semkey = "d_" + t.name + str(self.uid)
            self.uid += 1
            self.semh[t.semkey] = self.nc.alloc_semaphore(t.semkey)
            self.cnt[t.semkey] = 0
        return t.semkey

    def _deps(self, eng, reads, writes, nowaw):
        deps = {}
        for t in reads:
            for s, v in t.writers:
                if deps.get(s, 0) < v:
                    deps[s] = v
        for t in writes:
            for s, v in t.readers:
                if deps.get(s, 0) < v:
                    deps[s] = v
            if not nowaw:
                for s, v in t.writers:
                    if deps.get(s, 0) < v:
                        deps[s] = v
        e = self.eng[eng]
        seen = self.seen[eng]
        for s, v in deps.items():
            if eng == "pe" and s == "pe":
                continue
            if seen.get(s, 0) >= v:
                continue
            e.wait_ge(self.semh[s], v)
            seen[s] = v

    def _record(self, tok, reads, writes, nowaw):
        for t in writes:
            if nowaw and not t.readers:
                t.writers = [w for w in t.writers if w[0] != tok[0]] + [tok]
            else:
                t.writers = [tok]
            t.readers = []
        for t in reads:
            t.readers = [r for r in t.readers if r[0] != tok[0]] + [tok]

    def op(self, eng, fn, reads=(), writes=(), nowaw=False):
        self._deps(eng, reads, writes, nowaw)
        ins = fn(self.eng[eng])
        self.cnt[eng] += 1
        ins.then_inc(self.semh[eng], 1)
        tok = (eng, self.cnt[eng])
        self._record(tok, reads, writes, nowaw)
        return tok

    def dma(self, q, out_ap, in_ap, reads=(), writes=(), semtile=None, nowaw=False, **kw):
        self._deps(q, reads, writes, nowaw)
        st = semtile if semtile is not None else (writes[0] if writes else reads[0])
        s = self._dmasem(st)
        ins = self.eng[q].dma_start(out=out_ap, in_=in_ap, **kw)
        self.cnt[s] += 16
        ins.then_inc(self.semh[s], 16)
        tok = (s, self.cnt[s])
        self._record(tok, reads, writes, nowaw)
        return tok

    def finish(self, tiles):
        deps = {}
        for t in tiles:
            for s, v in t.writers:
                deps[s] = max(deps.get(s, 0), v)
        for s, v in deps.items():
            self.eng["sp"].wait_ge(self.semh[s], v)

    def mm(self, ps, lhsT, rhs, start, stop, reads):
        return self.op("pe", lambda e: e.matmul(ps_ap(ps), lhsT, rhs, start=start, stop=stop),
                       reads=reads, writes=[ps_t(ps)], nowaw=not start)


def ps_ap(ps):
    return ps[1] if isinstance(ps, tuple) else ps[:]


def ps_t(ps):
    return ps[0] if isinstance(ps, tuple) else ps


D_FF = 5632
XA_H = 4
MEM = 256
HALO = 16


class WStream:
    def __init__(self, k, nbuf=3, elems=11264):
        self.k = k
        self.bufs = [k.sb("wbuf%d" % i, [128, elems], BF16) for i in range(nbuf)]
        self.i = 0

    def load(self, W, kchunks, c0, ncols, r0=0):
        k = self.k
        buf = self.bufs[self.i % len(self.bufs)]
        self.i += 1
        view = buf[:, 0:kchunks * ncols].rearrange("p (c n) -> p c n", n=ncols)
        src = W.h[r0:r0 + kchunks * 128, c0:c0 + ncols].rearrange("(c p) n -> p c n", p=128)
        step = max(1, 4096 // ncols)
        first = True
        for a in range(0, kchunks, step):
            b = min(kchunks, a + step)
            k.dma("pool", view[:, a:b, :], src[:, a:b, :], reads=[W], writes=[buf], nowaw=not first)
            first = False
        return buf, view


def layer_norm_fm(k, cst, x, xb, T, lw, lb, li, sq2):
    ps_s = k.ps()
    ps_q = k.ps()
    ones = cst["ones_f"]
    for c in range(NCH):
        k.mm((ps_s, ps_s[:, :T]), ones[:, :], x[:, c, :T], c == 0, c == NCH - 1, [ones, x])
    for c in range(NCH):
        sq = sq2[c % 2]
        k.op("act", lambda e: e.activation(out=sq[:, :T], in_=x[:, c, :T], func=AF.Square), reads=[x], writes=[sq])
        k.mm((ps_q, ps_q[:, :T]), ones[:, :], sq[:, :T], c == 0, c == NCH - 1, [ones, sq])
    mean, rstd, tmp = cst["ln_mean"], cst["ln_rstd"], cst["ln_tmp"]
    k.op("act", lambda e: e.activation(out=mean[:, :T], in_=ps_s[:, :T], func=AF.Copy, scale=1.0 / D), reads=[ps_s], writes=[mean])
    k.op("dve", lambda e: e.tensor_tensor(out=tmp[:, :T], in0=mean[:, :T], in1=mean[:, :T], op=ALU.mult), reads=[mean], writes=[tmp])
    k.op("dve", lambda e: e.scalar_tensor_tensor(out=tmp[:, :T], in0=ps_q[:, :T], scalar=1.0 / D, in1=tmp[:, :T],
                                                  op0=ALU.mult, op1=ALU.subtract), reads=[ps_q, tmp], writes=[tmp])
    k.op("dve", lambda e: e.tensor_scalar(out=tmp[:, :T], in0=tmp[:, :T], scalar1=LN_EPS, scalar2=None, op0=ALU.add), reads=[tmp], writes=[tmp])
    k.op("act", lambda e: e.activation(out=tmp[:, :T], in_=tmp[:, :T], func=AF.Sqrt), reads=[tmp], writes=[tmp])
    k.op("dve", lambda e: e.reciprocal(out=rstd[:, :T], in_=tmp[:, :T]), reads=[tmp], writes=[rstd])
    for c in range(NCH):
        t2 = sq2[c % 2]
        k.op("dve", lambda e: e.tensor_tensor(out=t2[:, :T], in0=x[:, c, :T], in1=mean[:, :T], op=ALU.subtract), reads=[x, mean], writes=[t2])
        k.op("dve", lambda e: e.tensor_tensor(out=t2[:, :T], in0=t2[:, :T], in1=rstd[:, :T], op=ALU.mult), reads=[t2, rstd], writes=[t2])
        k.op("dve", lambda e: e.tensor_scalar(out=x[:, c, :T], in0=t2[:, :T], scalar1=lw[:, 2 * li, c:c + 1], scalar2=lw[:, 2 * li + 1, c:c + 1],
                                              op0=ALU.mult, op1=ALU.add), reads=[t2, lw], writes=[x])
        k.op("act", lambda e: e.activation(out=xb[:, c, :T], in_=x[:, c, :T], func=AF.Copy), reads=[x], writes=[xb], nowaw=True)


def proj_res(k, ws, W, rhs_b, x, T, kch, first=True):
    ncol = 512 if kch <= 16 else 256
    for ocg in range(D // ncol):
        buf, wv = ws.load(W, kch, ocg * ncol, ncol)
        for j in range(ncol // 128):
            oc = ocg * (ncol // 128) + j
            ps = k.ps()
            for kc in range(kch):
                k.mm((ps, ps[:, :T]), wv[:, kc, j * 128:(j + 1) * 128], rhs_b[:, kc, :T], kc == 0, kc == kch - 1, [buf, rhs_b])
            k.op("dve", lambda e: e.scalar_tensor_tensor(out=x[:, oc, :T], in0=x[:, oc, :T], scalar=(ALPHA if first else 1.0), in1=ps[:, :T],
                                                          op0=ALU.mult, op1=ALU.add), reads=[x, ps], writes=[x])


def build_stage_c(n_tiles, T=512, layer_last=True):
    k = KB()
    nc = k.nc
    NT = HALO + n_tiles * T
    xT = k.dram("xT", [D, NT], F32, "ExternalInput")
    yT = k.dram("yT", [D, NT], F32, "ExternalInput")
    memT = k.dram("memT", [D, MEM], F32, "ExternalInput")
    w_out = k.dram("w_out", [D, D], F32, "ExternalInput")
    w_q = k.dram("w_q", [D, D], F32, "ExternalInput")
    w_kv = k.dram("w_kv", [D, 2 * D], F32, "ExternalInput")
    w_o = k.dram("w_o", [D, D], F32, "ExternalInput")
    w_up = k.dram("w_up", [D, 2 * D_FF], F32, "ExternalInput")
    w_down = k.dram("w_down", [D_FF, D], F32, "ExternalInput")
    lnp = k.dram("lnp", [128, 6, NCH], F32, "ExternalInput")
    convw = k.dram("convw", [128, 88, 3], F32, "ExternalInput")
    halomask = k.dram("halomask", [128, 1], F32, "ExternalInput")
    outT = k.dram("outT", [D, n_tiles * T], F32, "ExternalOutput")

    k.psum_banks(8)
    cst = {}
    cst["ones_f"] = k.sb("ones_f", [128, 128], F32)
    cst["ones_b"] = k.sb("ones_b", [128, 128], BF16)
    cst["ln_mean"] = k.sb("ln_mean", [128, T], F32)
    cst["ln_rstd"] = k.sb("ln_rstd", [128, T], F32)
    cst["ln_tmp"] = k.sb("ln_tmp", [128, T], F32)
    k.op("dve", lambda e: e.memset(cst["ones_f"][:, :], 1.0), writes=[cst["ones_f"]])
    k.op("dve", lambda e: e.memset(cst["ones_b"][:, :], 1.0), writes=[cst["ones_b"]])
    sq2 = [k.sb("sq0", [128, T], F32), k.sb("sq1", [128, T], F32)]
    lnp_s = k.sb("lnp_s", [128, 6, NCH], F32)
    convw_s = k.sb("convw_s", [128, 88, 3], F32)
    hm_s = k.sb("hm_s", [128, 1], F32)
    k.dma("sp", lnp_s[:, :, :], lnp.h[:, :, :], reads=[lnp], writes=[lnp_s])
    k.dma("sp", convw_s[:, :, :], convw.h[:, :, :], reads=[convw], writes=[convw_s])
    k.dma("sp", hm_s[:, :], halomask.h[:, :], reads=[halomask], writes=[hm_s])
    lw = lnp_s
    x = k.sb("x", [128, NCH, T], F32)
    xb = k.sb("xb", [128, NCH, T], BF16)
    ab1 = k.sb("ab1", [128, NCH, T], BF16)
    big = k.sb("big", [128, 22, T], BF16)
    kT = k.sb("kT", [128, NCH, MEM], BF16)
    vv = k.sb("vv", [128, 2, D], BF16)
    memb = k.sb("memb", [128, NCH, MEM], BF16)
    hprev = k.sb("hprev", [128, 88, 2], F32)
    hg = k.sb("hg", [128, T + 2], F32)
    hu = k.sb("hu", [128, T + 2], F32)
    cg = k.sb("cg", [128, T], F32)
    cu = k.sb("cu", [128, T], F32)
    sg = k.sb("sg", [128, T], F32)
    pT = k.sb("pT", [128, 2, T], BF16)
    rinv = k.sb("rinv", [128, T], F32)
    qsq = k.sb("qsq", [128, T], BF16)
    negb = k.sb("negb", [1, T], BF16)
    nbf = k.sb("nbf", [1, T], F32)
    kss = k.sb("kss", [128, MEM], BF16)
    kmax2 = k.sb("kmax2", [128, XA_H], F32)
    ws = WStream(k, 3)
    ones_f, ones_b = cst["ones_f"], cst["ones_b"]

    k.dma("pool", memb[:, :, :], memT.h.rearrange("(c p) m -> p c m", p=128), reads=[memT], writes=[memb])
    for ocg in range(4):
        buf, wv = ws.load(w_kv, NCH, ocg * 512, 512)
        for j in range(4):
            oc = ocg * 4 + j
            ps = k.ps()
            for kc in range(NCH):
                k.mm((ps, ps[:, :MEM]), wv[:, kc, j * 128:(j + 1) * 128], memb[:, kc, :], kc == 0, kc == NCH - 1, [buf, memb])
            k.op("act", lambda e: e.activation(out=kT[:, oc, :], in_=ps[:, :MEM], func=AF.Copy), reads=[ps], writes=[kT], nowaw=True)
    for ocg in range(4):
        buf, wv = ws.load(w_kv, NCH, D + ocg * 512, 512)
        for mh in range(2):
            ps = k.ps()
            for kc in range(NCH):
                k.mm(ps, memb[:, kc, mh * 128:(mh + 1) * 128], wv[:, kc, :], kc == 0, kc == NCH - 1, [buf, memb])
            k.op("act", lambda e: e.activation(out=vv[:, mh, ocg * 512:(ocg + 1) * 512], in_=ps[:, :], func=AF.Copy), reads=[ps], writes=[vv], nowaw=True)
    for h in range(XA_H):
        ps = k.ps()
        for c in range(4):
            k.op("dve", lambda e: e.tensor_tensor(out=kss[:, :], in0=kT[:, h * 4 + c, :], in1=kT[:, h * 4 + c, :], op=ALU.mult), reads=[kT], writes=[kss])
            k.mm((ps, ps[:, :MEM]), ones_b[:, :], kss[:, :], c == 0, c == 3, [ones_b, kss])
        k.op("dve", lambda e: e.tensor_reduce(out=kmax2[:, h:h + 1], in_=ps[:, :MEM], axis=AX.X, op=ALU.max), reads=[ps], writes=[kmax2], nowaw=True)
    k.op("dve", lambda e: e.tensor_scalar(out=kmax2[:, :], in0=kmax2[:, :], scalar1=1.1, scalar2=None, op0=ALU.mult), reads=[kmax2], writes=[kmax2])

    xTv = xT.h.rearrange("(c p) t -> p c t", p=128)
    yTv = yT.h.rearrange("(c p) t -> p c t", p=128)
    oTv = outT.h.rearrange("(c p) t -> p c t", p=128)
    XS = float(512 ** -0.5)

    for ti in range(n_tiles + 1):
        halo = ti == 0
        Tt = HALO if halo else T
        t0 = 0 if halo else HALO + (ti - 1) * T
        for c0 in range(0, NCH, 4):
            k.dma("sp", x[:, c0:c0 + 4, :Tt], xTv[:, c0:c0 + 4, t0:t0 + Tt], reads=[xT], writes=[x], nowaw=c0 > 0)
            k.dma("pool", ab1[:, c0:c0 + 4, :Tt], yTv[:, c0:c0 + 4, t0:t0 + Tt], reads=[yT], writes=[ab1], nowaw=c0 > 0)
        proj_res(k, ws, w_out, ab1, x, Tt, NCH)
        _ln(k, cst, x, xb, Tt, lnp_s, 0, sq2)
        for ocg in range(4):
            buf, wv = ws.load(w_q, NCH, ocg * 512, 512)
            for j in range(4):
                oc = ocg * 4 + j
                ps = k.ps()
                for kc in range(NCH):
                    k.mm((ps, ps[:, :Tt]), wv[:, kc, j * 128:(j + 1) * 128], xb[:, kc, :Tt], kc == 0, kc == NCH - 1, [buf, xb])
                k.op("act", lambda e: e.activation(out=ab1[:, oc, :Tt], in_=ps[:, :Tt], func=AF.Copy, scale=XS), reads=[ps], writes=[ab1], nowaw=oc > 0)
        for h in range(XA_H):
            psb = k.ps()
            for c in range(4):
                k.op("dve", lambda e: e.tensor_tensor(out=qsq[:, :Tt], in0=ab1[:, h * 4 + c, :Tt], in1=ab1[:, h * 4 + c, :Tt], op=ALU.mult), reads=[ab1], writes=[qsq])
                k.mm((psb, psb[0:1, :Tt]), ones_b[:, 0:1], qsq[:, :Tt], c == 0, c == 3, [ones_b, qsq])
            k.op("act", lambda e: e.activation(out=nbf[0:1, :Tt], in_=psb[0:1, :Tt], func=AF.Sqrt, scale=kmax2[0:1, h:h + 1]), reads=[psb, kmax2], writes=[nbf])
            k.op("dve", lambda e: e.tensor_scalar(out=negb[0:1, :Tt], in0=nbf[0:1, :Tt], scalar1=-1.0, scalar2=None, op0=ALU.mult), reads=[nbf], writes=[negb])
            for mh in range(2):
                ps = k.ps()
                for c in range(4):
                    k.mm((ps, ps[:, :Tt]), kT[:, h * 4 + c, mh * 128:(mh + 1) * 128], ab1[:, h * 4 + c, :Tt], c == 0, False, [kT, ab1])
                k.mm((ps, ps[:, :Tt]), ones_b[0:1, :], negb[0:1, :Tt], False, True, [ones_b, negb])
                k.op("act", lambda e: e.activation(out=pT[:, mh, :Tt], in_=ps[:, :Tt], func=AF.Exp), reads=[ps], writes=[pT], nowaw=mh > 0)
            psl = k.ps()
            for mh in range(2):
                k.mm((psl, psl[:, :Tt]), ones_b[:, :], pT[:, mh, :Tt], mh == 0, mh == 1, [ones_b, pT])
            k.op("dve", lambda e: e.reciprocal(out=rinv[:, :Tt], in_=psl[:, :Tt]), reads=[psl], writes=[rinv])
            for c in range(4):
                pso = k.ps()
                for mh in range(2):
                    k.mm((pso, pso[:, :Tt]), vv[:, mh, (h * 4 + c) * 128:(h * 4 + c + 1) * 128], pT[:, mh, :Tt], mh == 0, mh == 1, [vv, pT])
                k.op("dve", lambda e: e.tensor_tensor(out=big[:, h * 4 + c, :Tt], in0=pso[:, :Tt], in1=rinv[:, :Tt], op=ALU.mult), reads=[pso, rinv], writes=[big], nowaw=(h * 4 + c) > 0)
        proj_res(k, ws, w_o, big, x, Tt, NCH)
        _ln(k, cst, x, xb, Tt, lnp_s, 1, sq2)
        for hf in range(2):
            for g in range(0, 22, 4):
                nj = min(4, 22 - g)
                bg, wg = ws.load(w_up, NCH, (hf * 22 + g) * 128, nj * 128)
                bu, wu = ws.load(w_up, NCH, D_FF + (hf * 22 + g) * 128, nj * 128)
                for j in range(nj):
                    ci = hf * 22 + g + j
                    for (bw, wv, hb, cc, cidx) in ((bg, wg, hg, cg, ci), (bu, wu, hu, cu, 44 + ci)):
                        ps = k.ps()
                        for kc in range(NCH):
                            k.mm((ps, ps[:, :Tt]), wv[:, kc, j * 128:(j + 1) * 128], xb[:, kc, :Tt], kc == 0, kc == NCH - 1, [bw, xb])
                        if halo:
                            k.op("act", lambda e: e.activation(out=hprev[:, cidx, :], in_=ps[:, Tt - 2:Tt], func=AF.Copy, scale=hm_s[:, 0:1]),
                                 reads=[ps, hm_s], writes=[hprev], nowaw=True)
                            continue
                        k.op("act", lambda e: e.activation(out=hb[:, 2:2 + Tt], in_=ps[:, :Tt], func=AF.Copy), reads=[ps], writes=[hb])
                        k.op("dve", lambda e: e.tensor_copy(out=hb[:, 0:2], in_=hprev[:, cidx, :]), reads=[hprev], writes=[hb], nowaw=True)
                        k.op("dve", lambda e: e.tensor_copy(out=hprev[:, cidx, :], in_=hb[:, Tt:Tt + 2]), reads=[hb], writes=[hprev])
                        k.op("dve", lambda e: e.tensor_scalar(out=cc[:, :Tt], in0=hb[:, 0:Tt], scalar1=convw_s[:, cidx, 0:1], scalar2=None, op0=ALU.mult),
                             reads=[hb, convw_s], writes=[cc])
                        k.op("dve", lambda e: e.scalar_tensor_tensor(out=cc[:, :Tt], in0=hb[:, 1:Tt + 1], scalar=convw_s[:, cidx, 1:2], in1=cc[:, :Tt],
                                                                      op0=ALU.mult, op1=ALU.add), reads=[hb, convw_s, cc], writes=[cc])
                        k.op("dve", lambda e: e.scalar_tensor_tensor(out=cc[:, :Tt], in0=hb[:, 2:Tt + 2], scalar=convw_s[:, cidx, 2:3], in1=cc[:, :Tt],
                                                                      op0=ALU.mult, op1=ALU.add), reads=[hb, convw_s, cc], writes=[cc])
                    if halo:
                        continue
                    k.op("act", lambda e: e.activation(out=sg[:, :Tt], in_=cg[:, :Tt], func=AF.Silu), reads=[cg], writes=[sg])
                    k.op("dve", lambda e: e.tensor_tensor(out=big[:, g + j, :Tt], in0=sg[:, :Tt], in1=cu[:, :Tt], op=ALU.mult), reads=[sg, cu], writes=[big],
                         nowaw=(g + j) > 0)
            if halo:
                continue
            for ocg in range(8):
                buf, wv = ws.load(w_down, 22, ocg * 256, 256, r0=hf * 22 * 128)
                for j in range(2):
                    oc = ocg * 2 + j
                    ps = k.ps()
                    for kc in range(22):
                        k.mm((ps, ps[:, :Tt]), wv[:, kc, j * 128:(j + 1) * 128], big[:, kc, :Tt], kc == 0, kc == 21, [buf, big])
                    k.op("dve", lambda e: e.scalar_tensor_tensor(out=x[:, oc, :Tt], in0=x[:, oc, :Tt], scalar=(ALPHA if hf == 0 else 1.0), in1=ps[:, :Tt],
                                                                  op0=ALU.mult, op1=ALU.add), reads=[x, ps], writes=[x])
        if halo:
            continue
        _ln(k, cst, x, xb, Tt, lnp_s, 2, sq2)
        for c0 in range(0, NCH, 4):
            k.dma("sp", oTv[:, c0:c0 + 4, t0 - HALO:t0 - HALO + Tt], x[:, c0:c0 + 4, :Tt], reads=[x], writes=[outT], semtile=x, nowaw=True)
    k.finish([outT])
    return k


def _ln(k, cst, x, xb, T, lnp_s, i, sq2):
    layer_norm_fm(k, cst, x, xb, T, lnp_s, None, i, sq2)


def _tt(k, out, a, b, op, r, w, eng="dve", nowaw=False):
    return k.op(eng, lambda e: e.tensor_tensor(out=out, in0=a, in1=b, op=op), reads=r, writes=w, nowaw=nowaw)


def _ts(k, out, a, s1, op0, r, w, s2=None, op1=None, eng="dve", nowaw=False):
    if op1 is None:
        return k.op(eng, lambda e: e.tensor_scalar(out=out, in0=a, scalar1=s1, scalar2=None, op0=op0), reads=r, writes=w, nowaw=nowaw)
    return k.op(eng, lambda e: e.tensor_scalar(out=out, in0=a, scalar1=s1, scalar2=s2, op0=op0, op1=op1), reads=r, writes=w, nowaw=nowaw)


def _stt(k, out, a, s, b, op0, op1, r, w, nowaw=False):
    return k.op("dve", lambda e: e.scalar_tensor_tensor(out=out, in0=a, scalar=s, in1=b, op0=op0, op1=op1), reads=r, writes=w, nowaw=nowaw)


def _act(k, out, a, func, r, w, scale=None, bias=None, nowaw=False):
    kw = {}
    if scale is not None:
        kw["scale"] = scale
    if bias is not None:
        kw["bias"] = bias
    return k.op("act", lambda e: e.activation(out=out, in_=a, func=func, **kw), reads=r, writes=w, nowaw=nowaw)


def sub(t, ap, name):
    return Tl(ap, name)


C = 64
G = 8
TB = 512


def neumann_inverse(k, cst, M0, N0, Q, Qt, P, psq, psqt, psp, TTb, niter):
    identG = cst["identG"]
    _tt(k, P[:, :], N0[:, :], identG[:, :], ALU.add, [N0, identG], [P])
    cq, cqt = N0, M0
    for it in range(niter):
        last = it == niter - 1
        nq, nqt = Q[it % 2], Qt[it % 2]
        for g in range(G):
            sl = slice(g * C, (g + 1) * C)
            k.mm((psqt, psqt[:, sl]), cq[:, sl], cqt[:, sl], True, True, [cq, cqt])
        _act(k, nqt[:, :], psqt[:, :], AF.Copy, [psqt], [nqt])
        if not last:
            for g in range(G):
                sl = slice(g * C, (g + 1) * C)
                k.mm((psq, psq[:, sl]), cqt[:, sl], cq[:, sl], True, True, [cq, cqt])
            _act(k, nq[:, :], psq[:, :], AF.Copy, [psq], [nq])
        for g in range(G):
            sl = slice(g * C, (g + 1) * C)
            k.mm((psp, psp[:, sl]), nqt[:, sl], P[:, sl], True, True, [nqt, P])
        if last:
            _tt(k, TTb[:, :], psp[:, :], P[:, :], ALU.add, [psp, P], [TTb])
        else:
            _tt(k, P[:, :], psp[:, :], P[:, :], ALU.add, [psp, P], [P])
        cq, cqt = nq, nqt


def build_b0(S):
    k = KB()
    NB = S // TB
    NW = 1284
    xT = k.dram("xT", [D, S], F32, "ExternalInput")
    wsel = k.dram("wsel", [D, NW], F32, "ExternalInput")
    convw = k.dram("convw", [128, 6, 4], F32, "ExternalInput")
    hpar = k.dram("hpar", [64, 4], F32, "ExternalInput")
    normw = k.dram("normw", [64, G, 128], F32, "ExternalInput")
    pw = k.dram("pw", [256, 256], F32, "ExternalInput")
    pscale = k.dram("pscale", [128, 2], F32, "ExternalInput")
    pinv = k.dram("pinv", [128, 4, 16], F32, "ExternalInput")
    prest = k.dram("prest", [128, 4], F32, "ExternalInput")
    cmat = k.dram("cmat", [128, 5, 128], F32, "ExternalInput")
    yb = k.dram("yb", [2, S, 128], F32, "ExternalOutput")
    ya = k.dram("ya", [256, S], F32, "ExternalOutput")

    psb = k.psum_banks(8)
    sb = k.sb
    cm = sb("cm", [128, 5, 128], F32)
    k.dma("sp", cm[:, :, :], cmat.h[:, :, :], reads=[cmat], writes=[cm])
    ident_b = sb("ident_b", [128, 128], BF16)
    _act(k, ident_b[:, :], cm[:, 0, :], AF.Copy, [cm], [ident_b])
    ones_f = sb("ones_f", [128, 128], F32)
    k.op("dve", lambda e: e.memset(ones_f[:, :], 1.0), writes=[ones_f])
    cst = {}
    identG = sb("identG", [C, G * C], F32)
    mstrG = sb("mstrG", [C, G * C], F32)
    minclG = sb("minclG", [C, G * C], F32)
    for g in range(G):
        _act(k, identG[:, g * C:(g + 1) * C], cm[0:C, 0, 0:C], AF.Copy, [cm], [identG], nowaw=True)
        _act(k, mstrG[:, g * C:(g + 1) * C], cm[0:C, 1, 0:C], AF.Copy, [cm], [mstrG], nowaw=True)
        _act(k, minclG[:, g * C:(g + 1) * C], cm[0:C, 2, 0:C], AF.Copy, [cm], [minclG], nowaw=True)
    cst["identG"] = identG
    W = sb("W", [128, NCH, NW], BF16)
    wv = wsel.h.rearrange("(c p) n -> p c n", p=128)
    for c0 in range(0, NCH, 2):
        k.dma("pool", W[:, c0:c0 + 2, :], wv[:, c0:c0 + 2, :], reads=[wsel], writes=[W], nowaw=c0 > 0)
    cw = sb("cw", [128, 6, 4], F32)
    k.dma("sp", cw[:, :, :], convw.h[:, :, :], reads=[convw], writes=[cw])
    hp = sb("hp", [64, 4], F32)
    k.dma("sp", hp[:, :], hpar.h[:, :], reads=[hpar], writes=[hp])
    nw = sb("nw", [64, G, 128], F32)
    k.dma("sp", nw[:, :, :], normw.h[:, :, :], reads=[normw], writes=[nw])
    pwb = sb("pwb", [128, 2, 256], BF16)
    k.dma("pool", pwb[:, :, :], pw.h.rearrange("(c p) n -> p c n", p=128), reads=[pw], writes=[pwb])
    psc = sb("psc", [128, 2], F32)
    k.dma("sp", psc[:, :], pscale.h[:, :], reads=[pscale], writes=[psc])
    pin = sb("pin", [128, 4, 16], F32)
    k.dma("sp", pin[:, :, :], pinv.h[:, :, :], reads=[pinv], writes=[pin])
    prs = sb("prs", [128, 4], F32)
    k.dma("sp", prs[:, :], prest.h[:, :], reads=[prest], writes=[prs])
    nea = sb("nea", [64, 2], F32)
    _act(k, nea[:, :], hp[:, 2:4], AF.Exp, [hp], [nea])
    _ts(k, nea[:, :], nea[:, :], -1.0, ALU.mult, [nea], [nea])

    xb2 = [sb("xb_a", [128, NCH, TB], BF16), sb("xb_b", [128, NCH, TB], BF16)]
    Z = [sb("Z%d" % h, [128, 128], F32) for h in range(2)]
    Zb = [sb("Zb%d" % h, [128, 128], BF16) for h in range(2)]
    for h in range(2):
        k.op("dve", lambda e: e.memset(Z[h][:, :], 0.0), writes=[Z[h]])
        k.op("dve", lambda e: e.memset(Zb[h][:, :], 0.0), writes=[Zb[h]])
    raw = [[sb("raw%d%d" % (h, i), [128, 3 + TB], F32) for i in range(3)] for h in range(2)]
    for h in range(2):
        for i in range(3):
            k.op("dve", lambda e: e.memset(raw[h][i][:, 0:3], 0.0), writes=[raw[h][i]])
    praw = [sb("praw%d" % i, [128, 16 + TB], F32) for i in range(2)]
    for i in range(2):
        k.op("dve", lambda e: e.memset(praw[i][:, 0:16], 0.0), writes=[praw[i]])
    cv = sb("cv", [128, TB], F32)
    sg = sb("sgb", [128, TB], F32)
    sqb = sb("sqb", [128, TB], F32)
    rn = sb("rn", [128, TB], F32)
    qTb = [sb("qTb%d" % h, [128, TB], BF16) for h in range(2)]
    kTb = [sb("kTb%d" % h, [128, TB], BF16) for h in range(2)]
    vTb = [sb("vTb%d" % h, [128, TB], BF16) for h in range(2)]
    ab = sb("ab", [C, G, 4], F32)
    gg = sb("gg", [C, G, 2], F32)
    bt = sb("bt", [C, G, 2], F32)
    nbt = sb("nbt", [C, G, 2], F32)
    gc = sb("gc", [C, G * 2], F32)
    egc = sb("egc", [C, G * 2], F32)
    edec = sb("edec", [C, G * 2], F32)
    x2s = sb("x2s", [C, G * 2], F32)
    diag = sb("diag", [C, G * C], F32)
    Rg = sb("Rg", [128, G * C], F32)
    egl = [sb("egl%d" % h, [128, G], F32) for h in range(2)]
    Dm = sb("Dm", [C, G * C], F32)
    Ds = sb("Ds", [C, G * C], F32)
    M0 = sb("M0", [C, G * C], F32)
    N0 = sb("N0", [C, G * C], F32)
    Qs = [sb("Q0", [C, G * C], F32), sb("Q1", [C, G * C], F32)]
    Qts = [sb("Qt0", [C, G * C], F32), sb("Qt1", [C, G * C], F32)]
    Pm = sb("Pm", [C, G * C], F32)
    TTb = [sb("TTb%d" % h, [C, G * C], BF16) for h in range(2)]
    intra = sb("intra", [C, G * C], BF16)
    intraT = [sb("intraT%d" % h, [C, G * C], BF16) for h in range(2)]
    X1 = sb("X1", [C, G, 128], BF16)
    X2 = sb("X2", [C, G, 128], BF16)
    K1 = [sb("K1%d" % h, [C, G, 128], BF16) for h in range(2)]
    U = [sb("U%d" % h, [C, G, 128], F32) for h in range(2)]
    WmT = [sb("WmT%d" % h, [128, G * C], BF16) for h in range(2)]
    zs = [sb("zs%d" % h, [C, G, 128], F32) for h in range(2)]
    ob = [sb("ob%d" % h, [C, G, 128], F32) for h in range(2)]
    Pb = [sb("Pb%d" % h, [C, 128], BF16) for h in range(2)]
    tmp4 = [sb("tmp4%d" % h, [C, 128], F32) for h in range(2)]
    osq = sb("osq", [C, G, 128], F32)
    oss = sb("oss", [C, G], F32)
    s2 = sb("s2", [128, 16 + TB], F32)
    s4 = sb("s4", [128, 16 + TB], F32)
    pm = sb("pm", [128, TB], F32)
    pmb = sb("pmb", [128, 2, TB], BF16)
    yas = sb("yas", [128, TB], F32)

    xTv = xT.h.rearrange("(c p) t -> p c t", p=128)
    yav = ya.h.rearrange("(c p) t -> p c t", p=128)
    pA = [psb[0], psb[1]]
    pC, pD, pE = psb[2], psb[3], psb[4]
    seqb = [(psb[5], psb[6]), (psb[7], psb[2])]
    npa = [0]

    def nextA():
        npa[0] += 1
        return pA[npa[0] % 2]

    def proj(xb, c0, ncol, T0=0, T1=TB):
        ps = nextA()
        for kc in range(NCH):
            k.mm((ps, ps[0:ncol, 0:T1 - T0]), W[:, kc, c0:c0 + ncol], xb[:, kc, T0:T1], kc == 0, kc == NCH - 1, [W, xb])
        return ps

    def body():
        for blk in range(NB):
            t0 = blk * TB
            xb = xb2[blk % 2]
            for c0 in range(0, NCH, 4):
                k.dma("pool", xb[:, c0:c0 + 4, :], xTv[:, c0:c0 + 4, t0:t0 + TB], reads=[xT], writes=[xb], nowaw=c0 > 0)
            for pc in range(2):
                pr = praw[pc]
                if blk > 0:
                    _act(k, pr[:, 0:16], pr[:, TB:TB + 16], AF.Copy, [pr], [pr])
                ps = proj(xb, pc * 128, 128)
                _act(k, pr[:, 16:16 + TB], ps[:, :], AF.Copy, [ps], [pr])
                L = TB + 15
                _tt(k, s2[:, 1:16 + TB], pr[:, 1:16 + TB], pr[:, 0:15 + TB], ALU.add, [pr], [s2])
                _stt(k, pm[:, :], s2[:, 16:16 + TB], prs[:, 0:1], pr[:, 16:16 + TB], ALU.mult, ALU.subtract, [s2, prs, pr], [pm])
                _tt(k, s4[:, 3:16 + TB], s2[:, 3:16 + TB], s2[:, 1:14 + TB], ALU.add, [s2], [s4])
                _stt(k, pm[:, :], s4[:, 16:16 + TB], prs[:, 1:2], pm[:, :], ALU.mult, ALU.add, [s4, prs, pm], [pm])
                if blk == 0:
                    _tt(k, cv[:, 0:16], s2[:, 16:32], pin[:, 0, :], ALU.mult, [s2, pin], [cv])
                    _tt(k, sg[:, 0:16], s4[:, 16:32], pin[:, 1, :], ALU.mult, [s4, pin], [sg])
                    _tt(k, cv[:, 0:16], cv[:, 0:16], sg[:, 0:16], ALU.add, [cv, sg], [cv])
                _tt(k, s2[:, 7:16 + TB], s4[:, 7:16 + TB], s4[:, 3:12 + TB], ALU.add, [s4], [s2])
                _stt(k, pm[:, :], s2[:, 16:16 + TB], prs[:, 2:3], pm[:, :], ALU.mult, ALU.add, [s2, prs, pm], [pm])
                if blk == 0:
                    _tt(k, sg[:, 0:16], s2[:, 16:32], pin[:, 2, :], ALU.mult, [s2, pin], [sg])
                    _tt(k, cv[:, 0:16], cv[:, 0:16], sg[:, 0:16], ALU.add, [cv, sg], [cv])
                _tt(k, s4[:, 15:16 + TB], s2[:, 15:16 + TB], s2[:, 7:8 + TB], ALU.add, [s2], [s4])
                _stt(k, pm[:, :], s4[:, 16:16 + TB], prs[:, 3:4], pm[:, :], ALU.mult, ALU.add, [s4, prs, pm], [pm])
                if blk == 0:
                    _tt(k, sg[:, 0:16], s4[:, 16:32], pin[:, 3, :], ALU.mult, [s4, pin], [sg])
                    _tt(k, cv[:, 0:16], cv[:, 0:16], sg[:, 0:16], ALU.add, [cv, sg], [cv])
                    _tt(k, pm[:, 0:16], cv[:, 0:16], pr[:, 16:32], ALU.subtract, [cv, pr], [pm])
                _act(k, pmb[:, pc, :], pm[:, :], AF.Copy, [pm], [pmb], nowaw=pc > 0)
            for oc in range(2):
                ps = nextA()
                for pc in range(2):
                    k.mm(ps, pwb[:, pc, oc * 128:(oc + 1) * 128], pmb[:, pc, :], pc == 0, pc == 1, [pwb, pmb])
                _ts(k, yas[:, :], ps[:, :], psc[:, oc:oc + 1], ALU.mult, [ps, psc], [yas])
                k.dma("sp", yav[:, oc, t0:t0 + TB], yas[:, :], reads=[yas], writes=[ya], semtile=yas, nowaw=True)
            ckpt('pool')
            psab = nextA()
            for g in range(G):
                for kc in range(NCH):
                    k.mm((psab, psab[0:C, g * 4:(g + 1) * 4]), xb[:, kc, g * C:(g + 1) * C], W[:, kc, 1280:1284], kc == 0, kc == NCH - 1, [W, xb])
            _act(k, ab[:, :, :], psab[0:C, 0:G * 4].rearrange("p (g f) -> p g f", f=4), AF.Copy, [psab], [ab])
            _tt(k, gg[:, :, :], ab[:, :, 0:2], hp[:, 0:2].unsqueeze(1).to_broadcast([C, G, 2]), ALU.add, [ab, hp], [gg])
            _act(k, gg[:, :, :], gg[:, :, :], AF.Exp, [gg], [gg])
            _act(k, gg[:, :, :], gg[:, :, :], AF.Ln, [gg], [gg], bias=1.0)
            _tt(k, gg[:, :, :], gg[:, :, :], nea[:, :].unsqueeze(1).to_broadcast([C, G, 2]), ALU.mult, [gg, nea], [gg])
            _act(k, bt[:, :, :], ab[:, :, 2:4], AF.Exp, [ab], [bt], scale=-1.0)
            _ts(k, bt[:, :, :], bt[:, :, :], 1.0, ALU.add, [bt], [bt])
            k.op("dve", lambda e: e.reciprocal(out=bt[:, :, :], in_=bt[:, :, :]), reads=[bt], writes=[bt])
            _ts(k, nbt[:, :, :], bt[:, :, :], -1.0, ALU.mult, [bt], [nbt])
            k.mm((pC, pC[0:C, 0:G * 2]), cm[0:C, 3, 0:C], gg[:, :, :].rearrange("p g h -> p (g h)"), True, True, [cm, gg])
            _act(k, gc[:, :], pC[0:C, 0:G * 2], AF.Copy, [pC], [gc])
            _act(k, egc[:, :], gc[:, :], AF.Exp, [gc], [egc])
            ckpt('gates')
            gc3 = gc[:, :].rearrange("p (g h) -> p g h", h=2)
            egc3 = egc[:, :].rearrange("p (g h) -> p g h", h=2)
            for h in range(2):
                base = 256 + h * 512
                for i, dst in enumerate((qTb[h], kTb[h], vTb[h])):
                    rw = raw[h][i]
                    if blk > 0:
                        _act(k, rw[:, 0:3], rw[:, TB:TB + 3], AF.Copy, [rw], [rw])
                    ps = proj(xb, base + i * 128, 128)
                    _act(k, rw[:, 3:3 + TB], ps[:, :], AF.Copy, [ps], [rw])
                    ci = h * 3 + i
                    _ts(k, cv[:, :], rw[:, 0:TB], cw[:, ci, 0:1], ALU.mult, [rw, cw], [cv])
                    for tap in range(1, 4):
                        _stt(k, cv[:, :], rw[:, tap:tap + TB], cw[:, ci, tap:tap + 1], cv[:, :], ALU.mult, ALU.add, [rw, cw, cv], [cv])
                    _act(k, sg[:, :], cv[:, :], AF.Silu, [cv], [sg])
                    if i == 2:
                        _act(k, dst[:, :], sg[:, :], AF.Copy, [sg], [dst])
                        continue
                    _act(k, sqb[:, :], sg[:, :], AF.Square, [sg], [sqb])
                    psn = nextA()
                    k.mm(psn, ones_f[:, :], sqb[:, :], True, True, [ones_f, sqb])
                    _act(k, rn[:, :], psn[:, :], AF.Sqrt, [psn], [rn], bias=1e-6)
                    k.op("dve", lambda e: e.reciprocal(out=rn[:, :], in_=rn[:, :]), reads=[rn], writes=[rn])
                    _stt(k, dst[:, :], sg[:, :], (128 ** -0.5 if i == 0 else 1.0), rn[:, :], ALU.mult, ALU.mult, [sg, rn], [dst])
                ckpt('conv')
                _tt(k, diag[:, :].rearrange("p (g j) -> p g j", j=C), identG[:, :].rearrange("p (g j) -> p g j", j=C),
                    gc3[:, :, h:h + 1].to_broadcast([C, G, C]), ALU.mult, [identG, gc], [diag])
                k.mm(pC, ones_f[0:C, :], diag[:, :], True, True, [ones_f, diag])
                _act(k, Rg[:, :], pC[:, :], AF.Copy, [pC], [Rg])
                Rg3 = Rg[:, :].rearrange("p (g j) -> p g j", j=C)
                _act(k, egl[h][:, :], Rg3[:, :, C - 1], AF.Exp, [Rg], [egl[h]])
                _tt(k, edec[:, :].rearrange("p (g h) -> p g h", h=2)[:, :, h], Rg3[0:C, :, C - 1], gc3[:, :, h], ALU.subtract, [Rg, gc], [edec], nowaw=True)
                _tt(k, Dm[:, :].rearrange("p (g j) -> p g j", j=C), Rg3[0:C, :, :], gc3[:, :, h:h + 1].to_broadcast([C, G, C]), ALU.subtract, [Rg, gc], [Dm])
                _ts(k, Dm[:, :], Dm[:, :], 0.0, ALU.max, [Dm], [Dm])
                _act(k, Dm[:, :], Dm[:, :], AF.Exp, [Dm], [Dm], scale=-1.0)
                _tt(k, Ds[:, :], Dm[:, :], mstrG[:, :], ALU.mult, [Dm, mstrG], [Ds])
                _tt(k, Dm[:, :], Dm[:, :], minclG[:, :], ALU.mult, [Dm, minclG], [Dm])
                ckpt('decay')
                for g in range(G):
                    sl = slice(g * C, (g + 1) * C)
                    k.mm((pD, pD[0:C, sl]), kTb[h][:, sl], kTb[h][:, sl], True, True, [kTb[h]])
                for g in range(G):
                    sl = slice(g * C, (g + 1) * C)
                    k.mm((pE, pE[0:C, sl]), qTb[h][:, sl], kTb[h][:, sl], True, True, [qTb[h], kTb[h]])
                _tt(k, M0[:, :], pD[0:C, :], Ds[:, :], ALU.mult, [pD, Ds], [M0])
                _tt(k, M0[:, :].rearrange("p (g j) -> p g j", j=C), M0[:, :].rearrange("p (g j) -> p g j", j=C),
                    nbt[:, :, h:h + 1].to_broadcast([C, G, C]), ALU.mult, [M0, nbt], [M0])
                _tt(k, intra[:, :], pE[0:C, :], Dm[:, :], ALU.mult, [pE, Dm], [intra])
                for g in range(G):
                    sl = slice(g * C, (g + 1) * C)
                    k.mm((pD, pD[0:C, sl]), M0[:, sl], cm[0:C, 0, 0:C], True, True, [M0, cm])
                _act(k, N0[:, :], pD[0:C, :], AF.Copy, [pD], [N0])
                for g in range(G):
                    sl = slice(g * C, (g + 1) * C)
                    k.mm((pE, pE[0:C, sl]), intra[:, sl], ident_b[0:C, 0:C], True, True, [intra, ident_b])
                _act(k, intraT[h][:, :], pE[0:C, :], AF.Copy, [pE], [intraT[h]])
                ckpt('gram')
                neumann_inverse(k, cst, M0, N0, Qs, Qts, Pm, _Half(pD), _Half(pE), _Half(pC), TTb[h], 5)
                ckpt('neumann')
                _tt(k, x2s[:, :].rearrange("p (g h) -> p g h", h=2)[:, :, h], nbt[:, :, h], egc3[:, :, h], ALU.mult, [nbt, egc], [x2s], nowaw=True)
                _act(k, edec[:, :].rearrange("p (g h) -> p g h", h=2)[:, :, h], edec[:, :].rearrange("p (g h) -> p g h", h=2)[:, :, h], AF.Exp, [edec], [edec])
                for half in range(2):
                    for gi in range(4):
                        g = half * 4 + gi
                        k.mm((pD, pD[0:C, gi * 128:(gi + 1) * 128]), kTb[h][:, g * C:(g + 1) * C], ident_b[:, :], True, True, [kTb[h], ident_b])
                        k.mm((pE, pE[0:C, gi * 128:(gi + 1) * 128]), vTb[h][:, g * C:(g + 1) * C], ident_b[:, :], True, True, [vTb[h], ident_b])
                    gs = slice(half * 4, half * 4 + 4)
                    kview = pD[0:C, :].rearrange("p (g d) -> p g d", d=128)
                    vview = pE[0:C, :].rearrange("p (g d) -> p g d", d=128)
                    x2v = x2s[:, :].rearrange("p (g h) -> p g h", h=2)
                    edv = edec[:, :].rearrange("p (g h) -> p g h", h=2)
                    _tt(k, X2[:, gs, :], kview, x2v[:, gs, h:h + 1].to_broadcast([C, 4, 128]), ALU.mult, [pD, x2s], [X2], nowaw=half > 0)
                    _tt(k, K1[h][:, gs, :], kview, edv[:, gs, h:h + 1].to_broadcast([C, 4, 128]), ALU.mult, [pD, edec], [K1[h]], nowaw=half > 0)
                    _tt(k, X1[:, gs, :], vview, bt[:, gs, h:h + 1].to_broadcast([C, 4, 128]), ALU.mult, [pE, bt], [X1], nowaw=half > 0)
                for half in range(2):
                    for gi in range(4):
                        g = half * 4 + gi
                        k.mm((pD, pD[0:C, gi * 128:(gi + 1) * 128]), TTb[h][:, g * C:(g + 1) * C], X1[:, g, :], True, True, [TTb[h], X1])
                    _act(k, U[h][:, half * 4:half * 4 + 4, :], pD[0:C, :].rearrange("p (g d) -> p g d", d=128), AF.Copy, [pD], [U[h]], nowaw=half > 0)
                for g in range(G):
                    k.mm((pE, pE[:, g * C:(g + 1) * C]), X2[:, g, :], TTb[h][:, g * C:(g + 1) * C], True, True, [X2, TTb[h]])
                _act(k, WmT[h][:, :], pE[:, :], AF.Copy, [pE], [WmT[h]])
                for half in range(2):
                    for gi in range(4):
                        g = half * 4 + gi
                        for kc in range(NCH):
                            k.mm((pD, pD[0:C, gi * 128:(gi + 1) * 128]), xb[:, kc, g * C:(g + 1) * C], W[:, kc, base + 384:base + 512], kc == 0, kc == NCH - 1, [xb, W])
                    _act(k, zs[h][:, half * 4:half * 4 + 4, :], pD[0:C, :].rearrange("p (g d) -> p g d", d=128), AF.Silu, [pD], [zs[h]], nowaw=half > 0)
            ckpt('prologue')
            for g in range(G):
                for h in range(2):
                    bA, bB = seqb[h]
                    sl = slice(g * C, (g + 1) * C)
                    k.mm((bA, bA[0:C, 0:128]), WmT[h][:, sl], Zb[h][:, :], True, True, [WmT[h], Zb[h]])
                    k.mm((bA, bA[0:C, 128:256]), qTb[h][:, sl], Zb[h][:, :], True, True, [qTb[h], Zb[h]])
                    _tt(k, Pb[h][:, :], bA[0:C, 0:128], U[h][:, g, :], ALU.add, [bA, U[h]], [Pb[h]])
                    k.mm((bB, bB[:, 128:256]), K1[h][:, g, :], Pb[h][:, :], True, True, [K1[h], Pb[h]])
                    k.mm((bB, bB[0:C, 0:128]), intraT[h][:, sl], Pb[h][:, :], True, True, [intraT[h], Pb[h]])
                    _stt(k, Z[h][:, :], Z[h][:, :], egl[h][:, g:g + 1], bB[:, 128:256], ALU.mult, ALU.add, [Z[h], egl[h], bB], [Z[h]])
                    _act(k, Zb[h][:, :], Z[h][:, :], AF.Copy, [Z[h]], [Zb[h]])
                    k.op("dve", lambda e: e.tensor_copy(out=tmp4[h][:, :], in_=bB[0:C, 0:128]), reads=[bB], writes=[tmp4[h]])
                    _stt(k, ob[h][:, g, :], bA[0:C, 128:256], egc[:, g * 2 + h:g * 2 + h + 1], tmp4[h][:, :], ALU.mult, ALU.add, [bA, egc, tmp4[h]], [ob[h]], nowaw=True)
            ckpt('seq')
            for h in range(2):
                _tt(k, osq[:, :, :], ob[h][:, :, :], ob[h][:, :, :], ALU.mult, [ob[h]], [osq])
                k.op("dve", lambda e: e.tensor_reduce(out=oss[:, :], in_=osq[:, :, :], axis=AX.X, op=ALU.add), reads=[osq], writes=[oss])
                _act(k, oss[:, :], oss[:, :], AF.Sqrt, [oss], [oss], scale=1.0 / 128, bias=1e-6)
                k.op("dve", lambda e: e.reciprocal(out=oss[:, :], in_=oss[:, :]), reads=[oss], writes=[oss])
                _tt(k, osq[:, :, :], ob[h][:, :, :], oss[:, :].unsqueeze(2).to_broadcast([C, G, 128]), ALU.mult, [ob[h], oss], [osq])
                _tt(k, osq[:, :, :], osq[:, :, :], nw[:, :, :], ALU.mult, [osq, nw], [osq])
                _tt(k, osq[:, :, :], osq[:, :, :], zs[h][:, :, :], ALU.mult, [osq, zs[h]], [osq])
                k.dma("sp", yb.h[h, t0:t0 + TB, :].rearrange("(g p) d -> p g d", p=C), osq[:, :, :], reads=[osq], writes=[yb], semtile=osq, nowaw=True)
    try:
        body()
    except StopBuild:
        pass
    k.finish([yb, ya])
    return k


class _Half:
    def __init__(self, t):
        self.t = t

    @property
    def writers(self):
        return self.t.writers

    @writers.setter
    def writers(self, v):
        self.t.writers = v

    @property
    def readers(self):
        return self.t.readers

    @readers.setter
    def readers(self, v):
        self.t.readers = v

    def __getitem__(self, idx):
        return self.t.h[0:C][idx] if not isinstance(idx, tuple) else self.t.h[(slice(0, C),) + tuple(idx[1:])]


def _cmat():
    m = np.zeros((128, 5, 128), np.float32)
    m[:, 4, :] = np.triu(np.ones((128, 128)), 1)
    m[:, 0, :] = np.eye(128)
    m[:, 1, :] = np.tril(np.ones((128, 128)), -1)
    m[:, 2, :] = np.tril(np.ones((128, 128)), 0)
    m[:, 3, :] = np.triu(np.ones((128, 128)), 0)
    return m


def prep_b0(inp, xT_b, hp_i):
    w = inp["ev_w_in"][0]
    cols = list(range(hp_i * 256, (hp_i + 1) * 256))
    for h in (2 * hp_i, 2 * hp_i + 1):
        for blk in range(4):
            cols += list(range(1024 * (1 + blk) + h * 128, 1024 * (1 + blk) + (h + 1) * 128))
    cols += [5120 + 2 * hp_i, 5120 + 2 * hp_i + 1, 5128 + 2 * hp_i, 5128 + 2 * hp_i + 1]
    wsel = np.ascontiguousarray(w[:, cols])
    cw = inp["gdn_conv_w"][0]
    convw = np.zeros((128, 6, 4), np.float32)
    for hh in range(2):
        h = 2 * hp_i + hh
        for i in range(3):
            convw[:, hh * 3 + i, :] = cw[:, i * 1024 + h * 128:i * 1024 + (h + 1) * 128].T
    hpar = np.zeros((64, 4), np.float32)
    hpar[:, 0:2] = inp["gdn_dt_bias"][0][2 * hp_i:2 * hp_i + 2][None, :]
    hpar[:, 2:4] = inp["gdn_a_log"][0][2 * hp_i:2 * hp_i + 2][None, :]
    normw = np.ascontiguousarray(np.broadcast_to(inp["gdn_norm_w"][0][None, None, :], (64, G, 128))).astype(np.float32)
    pw = np.ascontiguousarray(inp["pool_w"][0][hp_i])
    pscale = np.ascontiguousarray(inp["pool_scale"][0][hp_i * 256:(hp_i + 1) * 256].reshape(2, 128).T)
    wins = (2, 4, 8, 16)
    pinv = np.zeros((128, 4, 16), np.float32)
    prest = np.zeros((128, 4), np.float32)
    pos = np.arange(1, 17, dtype=np.float32)
    pinv[:, hp_i, :] = (1.0 / np.minimum(pos, wins[hp_i]))[None, :]
    prest[:, hp_i] = 1.0 / wins[hp_i]
    return dict(xT=xT_b, wsel=wsel, convw=convw, hpar=hpar, normw=normw, pw=pw, pscale=pscale, pinv=pinv, prest=prest, cmat=_cmat())


NW1 = 1826
HR = 64


def build_b1(S, do_rwkv=True, do_fox=True):
    k = KB()
    NB = S // TB
    xT = k.dram("xT", [D, S], F32, "ExternalInput")
    wsel = k.dram("wsel", [D, NW1], F32, "ExternalInput")
    mus = k.dram("mus", [128, 16], F32, "ExternalInput")
    hch = k.dram("hch", [64, 4, 5], F32, "ExternalInput")
    w2 = k.dram("w2", [64, 256], F32, "ExternalInput")
    a2 = k.dram("a2", [64, 256], F32, "ExternalInput")
    g2 = k.dram("g2", [160, 256], F32, "ExternalInput")
    lnx = k.dram("lnx", [64, 2, 4, 64], F32, "ExternalInput")
    fbf = k.dram("fbf", [128, 2], F32, "ExternalInput")
    cmat = k.dram("cmat", [128, 5, 128], F32, "ExternalInput")
    yc = k.dram("yc", [4, S, 64], F32, "ExternalOutput")
    yd = k.dram("yd", [2, S, 128], F32, "ExternalOutput")

    psb = k.psum_banks(8)
    sb = k.sb
    cm = sb("cm", [128, 5, 128], F32)
    k.dma("sp", cm[:, :, :], cmat.h[:, :, :], reads=[cmat], writes=[cm])
    ident_b = sb("ident_b", [128, 128], BF16)
    _act(k, ident_b[:, :], cm[:, 0, :], AF.Copy, [cm], [ident_b])
    ones_f = sb("ones_f", [128, 128], F32)
    k.op("dve", lambda e: e.memset(ones_f[:, :], 1.0), writes=[ones_f])
    ones_b = sb("ones_b", [128, 128], BF16)
    k.op("dve", lambda e: e.memset(ones_b[:, :], 1.0), writes=[ones_b])
    W = sb("W", [128, NCH, 1056], BF16)
    wv = wsel.h.rearrange("(c p) n -> p c n", p=128)

    def load_w(c0, n):
        for a in range(0, NCH, 2):
            k.dma("pool", W[:, a:a + 2, 0:n], wv[:, a:a + 2, c0:c0 + n], reads=[wsel], writes=[W], nowaw=a > 0)
    xb2 = [sb("xb_a", [128, NCH, TB], BF16)]
    xTv = xT.h.rearrange("(c p) t -> p c t", p=128)
    pA = [psb[0], psb[1]]
    npa = [0]

    def nextA():
        npa[0] += 1
        return pA[npa[0] % 2]

    def proj(xb, c0, ncol, T0=0, T1=TB):
        ps = nextA()
        for kc in range(NCH):
            k.mm((ps, ps[0:ncol, 0:T1 - T0]), W[:, kc, c0:c0 + ncol], xb[:, kc, T0:T1], kc == 0, kc == NCH - 1, [W, xb])
        return ps

    def load_x(blk, i):
        xb = xb2[0]
        for c0 in range(0, NCH, 4):
            k.dma("pool", xb[:, c0:c0 + 4, :], xTv[:, c0:c0 + 4, blk * TB:(blk + 1) * TB], reads=[xT], writes=[xb], nowaw=c0 > 0)
        return xb

    xi = [0]
    if do_rwkv:
        load_w(0, 1056)
        with contextlib.ExitStack() as st:
            k.scope = st
            _b1_rwkv(k, locals())
        k.scope = None
    if do_fox:
        load_w(1056, 770)
        with contextlib.ExitStack() as st:
            k.scope = st
            _b1_fox(k, locals())
        k.scope = None
    k.finish([yc, yd])
    return k


def _b1_rwkv(k, L):
    sb, psb, cm, ident_b, ones_f, ones_b, W, proj, load_x, nextA = (L[n] for n in ("sb", "psb", "cm", "ident_b", "ones_f", "ones_b", "W", "proj", "load_x", "nextA"))
    NB, yc, xi = L["NB"], L["yc"], L["xi"]
    H = HR
    mu = sb("mu", [128, 16], F32)
    k.dma("sp", mu[:, :], L["mus"].h[:, :], reads=[L["mus"]], writes=[mu])
    hc = sb("hc", [64, 4, 5], F32)
    k.dma("sp", hc[:, :, :], L["hch"].h[:, :, :], reads=[L["hch"]], writes=[hc])
    w2b = sb("w2b", [64, 256], BF16)
    a2b = sb("a2b", [64, 256], BF16)
    g2a = sb("g2a", [128, 256], BF16)
    g2b = sb("g2b", [32, 256], BF16)
    k.dma("pool", w2b[:, :], L["w2"].h[:, :], reads=[L["w2"]], writes=[w2b])
    k.dma("pool", a2b[:, :], L["a2"].h[:, :], reads=[L["a2"]], writes=[a2b])
    k.dma("pool", g2a[:, :], L["g2"].h[0:128, :], reads=[L["g2"]], writes=[g2a])
    k.dma("pool", g2b[:, :], L["g2"].h[128:160, :], reads=[L["g2"]], writes=[g2b])
    lnxs = sb("lnxs", [64, 2, 4, 64], F32)
    k.dma("sp", lnxs[:, :, :, :], L["lnx"].h[:, :, :, :], reads=[L["lnx"]], writes=[lnxs])
    identG = sb("identG", [C, G * C], F32)
    mstrG = sb("mstrG", [C, G * C], F32)
    mJT = sb("mJT", [C, 4, 128], F32)
    for g in range(G):
        _act(k, identG[:, g * C:(g + 1) * C], cm[0:C, 0, 0:C], AF.Copy, [cm], [identG], nowaw=True)
        _act(k, mstrG[:, g * C:(g + 1) * C], cm[0:C, 1, 0:C], AF.Copy, [cm], [mstrG], nowaw=True)
    for g in range(4):
        _act(k, mJT[:, g, 0:C], cm[0:C, 4, 0:C], AF.Copy, [cm], [mJT], nowaw=True)
        _act(k, mJT[:, g, C:2 * C], cm[0:C, 3, 0:C], AF.Copy, [cm], [mJT], nowaw=True)
    cst = {"identG": identG}
    onesrow = sb("onesrow", [64, C], F32)
    k.op("dve", lambda e: e.memset(onesrow[:, :], 1.0), writes=[onesrow])
    rawb = sb("rawb", [128, 1 + TB], F32)
    lastc = sb("lastc", [128, 16], F32)
    k.op("dve", lambda e: e.memset(lastc[:, :], 0.0), writes=[lastc])
    twl = sb("twl", [64, TB], BF16)
    cal = sb("cal", [64, TB], BF16)
    sgl = sb("sgl", [128, TB], BF16)
    sgl2 = sb("sgl2", [32, TB], BF16)
    tA = sb("tA", [128, TB], F32)
    f = lambda n, dt=F32: sb(n, [64, TB], dt)
    rc, kc_, vc, ld, av, kk, kp, t1, t2, cs, ecs, encs, eprev, erel = (f(n) for n in ("rc", "kc_", "vc", "ld", "av", "kk", "kp", "t1", "t2", "cs", "ecs", "encs", "eprev", "erel"))
    bb = f("bb")
    bt_, kt_ = f("bt_"), f("kt_")
    AR = sb("AR", [64, G, 128], F32)
    rtb = [f("rtb%d" % j, BF16) for j in range(2)]
    atb, vb16, Bhb, Khb = f("atb", BF16), f("vb16", BF16), f("Bhb", BF16), f("Khb", BF16)
    prod = f("prod")
    clv = sb("clv", [64, G], F32)
    ecl = [sb("ecl%d" % j, [64, G], F32) for j in range(2)]
    NA = sb("NA", [64, G, 128], F32)
    KA = sb("KA", [64, G, 128], BF16)
    M0 = sb("M0", [C, G * C], F32)
    N0 = sb("N0", [C, G * C], F32)
    A1Tb = [sb("A1Tb%d" % j, [C, G * C], BF16) for j in range(2)]
    Qs = [sb("Q0", [C, G * C], F32), sb("Q1", [C, G * C], F32)]
    Qts = [sb("Qt0", [C, G * C], F32), sb("Qt1", [C, G * C], F32)]
    Pm = sb("Pm", [C, G * C], F32)
    TTb = sb("TTb", [C, G * C], BF16)
    X2 = sb("X2", [C, G, H], BF16)
    Vtm = sb("Vtm", [C, G, H], BF16)
    Vtf = sb("Vtf", [C, G, H], F32)
    K1 = [sb("K1%d" % j, [C, G, H], BF16) for j in range(2)]
    K2 = sb("K2", [C, G, H], BF16)
    Y1b = sb("Y1b", [C, G, H], BF16)
    U = [sb("U%d" % j, [C, G, H], F32) for j in range(2)]
    WmT = [sb("WmT%d" % j, [H, G * C], BF16) for j in range(2)]
    O2 = [sb("O2%d" % j, [C, G, H], F32) for j in range(2)]
    KV2 = [sb("KV2%d" % j, [H, G, H], F32) for j in range(2)]
    bon = [sb("bon%d" % j, [C, G, H], F32) for j in range(2)]
    gate = [sb("gate%d" % j, [C, G, H], F32) for j in range(2)]
    ob = [sb("ob%d" % j, [C, G, H], F32) for j in range(2)]
    Z = [sb("Z%d" % j, [H, H], F32) for j in range(4)]
    Zb = [sb("Zb%d" % j, [H, H], BF16) for j in range(4)]
    Pb = [sb("Pb%d" % j, [C, H], BF16) for j in range(2)]
    for j in range(4):
        k.op("dve", lambda e: e.memset(Z[j][:, :], 0.0), writes=[Z[j]])
        k.op("dve", lambda e: e.memset(Zb[j][:, :], 0.0), writes=[Zb[j]])
    A1Tb, K1, U, WmT, O2, KV2, bon, gate, ob, Pb, rtb, ecl = (M2(x) for x in (A1Tb, K1, U, WmT, O2, KV2, bon, gate, ob, Pb, rtb, ecl))
    gm = sb("gm", [C, G], F32)
    gx = sb("gx", [C, G, H], F32)
    gq = sb("gq", [C, G, H], F32)
    pC, pD = psb[2], psb[3]
    hC, hD = _Half(pC), _Half(pD)
    hA = [_Half(psb[0]), _Half(psb[1])]
    seqb = [psb[4], psb[5], psb[6], psb[7]]

    def shift(raw_t, ps, P, mucol, out, func=None, blk=0):
        raw_t = rawb
        _act(k, raw_t[0:P, 0:1], lastc[0:P, mucol:mucol + 1], AF.Copy, [lastc], [raw_t])
        _act(k, raw_t[0:P, 1:1 + TB], ps[0:P, :], AF.Copy, [ps], [raw_t], nowaw=True)
        _act(k, lastc[0:P, mucol:mucol + 1], raw_t[0:P, TB:TB + 1], AF.Copy, [raw_t], [lastc])
        _tt(k, tA[0:P, :], raw_t[0:P, 0:TB], raw_t[0:P, 1:1 + TB], ALU.subtract, [raw_t], [tA])
        if func is None:
            _stt(k, out[0:P, :], tA[0:P, :], mu[0:P, mucol:mucol + 1], raw_t[0:P, 1:1 + TB], ALU.mult, ALU.add, [tA, mu, raw_t], [out])
        else:
            _stt(k, tA[0:P, :], tA[0:P, :], mu[0:P, mucol:mucol + 1], raw_t[0:P, 1:1 + TB], ALU.mult, ALU.add, [tA, mu, raw_t], [tA])
            _act(k, out[0:P, :], tA[0:P, :], func, [tA], [out])

    v3 = lambda t: t[:, :].rearrange("p (g j) -> p g j", j=C)
    for blk in range(NB):
        t0 = blk * TB
        xb = load_x(blk, xi[0])
        xi[0] += 1
        shift(None, proj(xb, 768, 64), 64, 12, twl, AF.Tanh, blk)
        shift(None, proj(xb, 832, 64), 64, 13, cal, AF.Copy, blk)
        shift(None, proj(xb, 896, 128), 128, 14, sgl, AF.Sigmoid, blk)
        shift(None, proj(xb, 1024, 32), 32, 15, sgl2, AF.Sigmoid, blk)
        for pair in range(2):
            js = (2 * pair, 2 * pair + 1)
            for j in js:
                for i, dst in enumerate((rc, kc_, vc)):
                    shift(None, proj(xb, j * 192 + i * 64, 64), 64, j * 3 + i, dst, None, blk)
                ps = nextA()
                k.mm((ps, ps[0:H, :]), w2b[:, j * H:(j + 1) * H], twl[:, :], True, True, [w2b, twl])
                _act(k, ld[:, :], ps[0:H, :], AF.Sigmoid, [ps, hc], [ld], bias=hc[:, j, 0:1])
                _ts(k, ld[:, :], ld[:, :], -float(np.exp(-0.5)), ALU.mult, [ld], [ld])
                ps = nextA()
                k.mm((ps, ps[0:H, :]), a2b[:, j * H:(j + 1) * H], cal[:, :], True, True, [a2b, cal])
                _act(k, av[:, :], ps[0:H, :], AF.Sigmoid, [ps, hc], [av], bias=hc[:, j, 1:2])
                _ts(k, t1[:, :], kc_[:, :], hc[:, j, 2:3], ALU.mult, [kc_, hc], [t1])
                _tt(k, t2[:, :], t1[:, :], t1[:, :], ALU.mult, [t1], [t2])
                ps = nextA()
                k.mm((ps, ps[0:H, :]), ones_f[0:H, 0:H], t2[:, :], True, True, [ones_f, t2])
                _act(k, t2[:, :], ps[0:H, :], AF.Sqrt, [ps], [t2], bias=1e-6)
                k.op("dve", lambda e: e.reciprocal(out=t2[:, :], in_=t2[:, :]), reads=[t2], writes=[t2])
                _tt(k, kk[:, :], t1[:, :], t2[:, :], ALU.mult, [t1, t2], [kk])
                _ts(k, t1[:, :], av[:, :], -1.0, ALU.add, [av, hc], [t1], s2=hc[:, j, 3:4], op1=ALU.mult)
                _stt(k, kp[:, :], t1[:, :], 1.0, kc_[:, :], ALU.add, ALU.mult, [t1, kc_], [kp])
                _tt(k, bb[:, :], kk[:, :], av[:, :], ALU.mult, [kk, av], [bb])
                _stt(k, prod[:, :], rc[:, :], hc[:, j, 4:5], kp[:, :], ALU.mult, ALU.mult, [rc, hc, kp], [prod])
                for g in range(G):
                    sl = slice(g * C, (g + 1) * C)
                    k.op("dve", lambda e: e.tensor_tensor_scan(out=cs[:, sl], data0=onesrow[:, :], data1=ld[:, sl], initial=0.0, op0=ALU.mult, op1=ALU.add),
                         reads=[onesrow, ld], writes=[cs], nowaw=g > 0)
                _act(k, ecs[:, :], cs[:, :], AF.Exp, [cs], [ecs])
                _act(k, encs[:, :], cs[:, :], AF.Exp, [cs], [encs], scale=-1.0)
                _tt(k, t1[:, :], cs[:, :], ld[:, :], ALU.subtract, [cs, ld], [t1])
                _act(k, eprev[:, :], t1[:, :], AF.Exp, [t1], [eprev])
                _act(k, clv[:, :], v3(cs)[:, :, C - 1], AF.Copy, [cs], [clv])
                _act(k, ecl[j][:, :], clv[:, :], AF.Exp, [clv], [ecl[j]])
                _tt(k, v3(t1), clv[:, :].unsqueeze(2).to_broadcast([H, G, C]), v3(cs), ALU.subtract, [clv, cs], [t1])
                _act(k, erel[:, :], t1[:, :], AF.Exp, [t1], [erel])
                _tt(k, AR[:, :, C:2 * C], v3(rc), v3(ecs), ALU.mult, [rc, ecs], [AR])
                _stt(k, AR[:, :, 0:C], v3(kk), -1.0, v3(eprev), ALU.mult, ALU.mult, [kk, eprev], [AR], nowaw=True)
                _tt(k, kt_[:, :], kp[:, :], encs[:, :], ALU.mult, [kp, encs], [kt_])
                _tt(k, bt_[:, :], bb[:, :], encs[:, :], ALU.mult, [bb, encs], [bt_])
                _tt(k, Khb[:, :], kp[:, :], erel[:, :], ALU.mult, [kp, erel], [Khb])
                _tt(k, Bhb[:, :], bb[:, :], erel[:, :], ALU.mult, [bb, erel], [Bhb])
                _act(k, v3(rtb[j]), AR[:, :, C:2 * C], AF.Copy, [AR], [rtb[j]])
                _act(k, v3(atb), AR[:, :, 0:C], AF.Copy, [AR], [atb])
                _act(k, vb16[:, :], vc[:, :], AF.Copy, [vc], [vb16])
                for half in range(2):
                    for gi in range(4):
                        g = half * 4 + gi
                        sl = slice(g * C, (g + 1) * C)
                        k.mm((hC, hC[:, gi * 128:(gi + 1) * 128]), bt_[:, sl], AR[:, g, :], True, True, [bt_, AR])
                        k.mm((hD, hD[:, gi * 128:(gi + 1) * 128]), kt_[:, sl], AR[:, g, :], True, True, [kt_, AR])
                    gs = slice(half * 4, half * 4 + 4)
                    _tt(k, NA[:, gs, :], hC[:, :].rearrange("p (g d) -> p g d", d=128), mJT[:, :, :], ALU.mult, [hC, mJT], [NA], nowaw=half > 0)
                    _tt(k, KA[:, gs, :], hD[:, :].rearrange("p (g d) -> p g d", d=128), mJT[:, :, :], ALU.mult, [hD, mJT], [KA], nowaw=half > 0)
                for g in range(G):
                    sl = slice(g * C, (g + 1) * C)
                    k.mm((hC, hC[:, sl]), AR[:, g, 0:C], bt_[:, sl], True, True, [AR, bt_])
                _tt(k, M0[:, :], hC[:, :], mstrG[:, :], ALU.mult, [hC, mstrG], [M0])
                _act(k, v3(N0), NA[:, :, 0:C], AF.Copy, [NA], [N0])
                _act(k, v3(A1Tb[j]), NA[:, :, C:2 * C], AF.Copy, [NA], [A1Tb[j]])
                neumann_inverse(k, cst, M0, N0, Qs, Qts, Pm, hC, hD, hA[0], TTb, 5)
                for src, dsts in ((atb, [X2]), (vb16, [Vtm, Vtf]), (Bhb, [K1[j]]), (Khb, [K2])):
                    for g in range(G):
                        sl = slice(g * C, (g + 1) * C)
                        k.mm((hC, hC[:, sl]), src[:, sl], ident_b[0:H, 0:H], True, True, [src, ident_b])
                    for d_ in dsts:
                        _act(k, d_[:, :, :], v3(hC), AF.Copy, [hC], [d_])
                for g in range(G):
                    sl = slice(g * C, (g + 1) * C)
                    k.mm((hC, hC[:, sl]), KA[:, g, 0:C], Vtm[:, g, :], True, True, [KA, Vtm])
                    k.mm((hD, hD[:, sl]), KA[:, g, C:2 * C], Vtm[:, g, :], True, True, [KA, Vtm])
                _act(k, Y1b[:, :, :], v3(hC), AF.Copy, [hC], [Y1b])
                _act(k, O2[j][:, :, :], v3(hD), AF.Copy, [hD], [O2[j]])
                for g in range(G):
                    sl = slice(g * C, (g + 1) * C)
                    k.mm((hC, hC[:, sl]), TTb[:, sl], Y1b[:, g, :], True, True, [TTb, Y1b])
                    k.mm((hD, hD[:, sl]), X2[:, g, :], TTb[:, sl], True, True, [X2, TTb])
                _act(k, U[j][:, :, :], v3(hC), AF.Copy, [hC], [U[j]])
                _act(k, WmT[j][:, :], hD[:, :], AF.Copy, [hD], [WmT[j]])
                for g in range(G):
                    sl = slice(g * C, (g + 1) * C)
                    k.mm((hC, hC[:, sl]), K2[:, g, :], Vtm[:, g, :], True, True, [K2, Vtm])
                    k.mm((hD, hD[:, sl]), prod[:, sl], ones_f[0:H, 0:H], True, True, [prod, ones_f])
                _act(k, KV2[j][:, :, :], v3(hC), AF.Copy, [hC], [KV2[j]])
                _tt(k, bon[j][:, :, :], v3(hD), Vtf[:, :, :], ALU.mult, [hD, Vtf], [bon[j]])
                for g in range(G):
                    sl = slice(g * C, (g + 1) * C)
                    k.mm((hC, hC[:, sl]), sgl[:, sl], g2a[:, j * H:(j + 1) * H], True, False, [sgl, g2a])
                    k.mm((hC, hC[:, sl]), sgl2[:, sl], g2b[:, j * H:(j + 1) * H], False, True, [sgl2, g2b])
                _act(k, gate[j][:, :, :], v3(hC), AF.Copy, [hC], [gate[j]])
            for g in range(G):
                sl = slice(g * C, (g + 1) * C)
                for j in js:
                    bk = seqb[j]
                    k.mm((bk, bk[0:C, 0:H]), WmT[j][:, sl], Zb[j][:, :], True, True, [WmT[j], Zb[j]])
                    k.mm((bk, bk[0:C, H:2 * H]), rtb[j][:, sl], Zb[j][:, :], True, True, [rtb[j], Zb[j]])
                    _tt(k, Pb[j][:, :], bk[0:C, 0:H], U[j][:, g, :], ALU.add, [bk, U[j]], [Pb[j]])
                    _tt(k, ob[j][:, g, :], bk[0:C, H:2 * H], O2[j][:, g, :], ALU.add, [bk, O2[j]], [ob[j]], nowaw=True)
                    k.mm((bk, bk[0:H, 2 * H:3 * H]), K1[j][:, g, :], Pb[j][:, :], True, True, [K1[j], Pb[j]])
                    k.mm((bk, bk[0:C, 3 * H:4 * H]), A1Tb[j][:, sl], Pb[j][:, :], True, True, [A1Tb[j], Pb[j]])
                    _stt(k, Z[j][:, :], Z[j][:, :], ecl[j][:, g:g + 1], bk[0:H, 2 * H:3 * H], ALU.mult, ALU.add, [Z[j], ecl[j], bk], [Z[j]])
                    _tt(k, ob[j][:, g, :], ob[j][:, g, :], bk[0:C, 3 * H:4 * H], ALU.add, [ob[j], bk], [ob[j]])
                    _tt(k, Z[j][:, :], Z[j][:, :], KV2[j][:, g, :], ALU.add, [Z[j], KV2[j]], [Z[j]])
                    _act(k, Zb[j][:, :], Z[j][:, :], AF.Copy, [Z[j]], [Zb[j]])
            for j in js:
                k.op("dve", lambda e: e.tensor_reduce(out=gm[:, :], in_=ob[j][:, :, :], axis=AX.X, op=ALU.add), reads=[ob[j]], writes=[gm])
                _ts(k, gm[:, :], gm[:, :], 1.0 / H, ALU.mult, [gm], [gm])
                _tt(k, gx[:, :, :], ob[j][:, :, :], gm[:, :].unsqueeze(2).to_broadcast([C, G, H]), ALU.subtract, [ob[j], gm], [gx])
                _tt(k, gq[:, :, :], gx[:, :, :], gx[:, :, :], ALU.mult, [gx], [gq])
                k.op("dve", lambda e: e.tensor_reduce(out=gm[:, :], in_=gq[:, :, :], axis=AX.X, op=ALU.add), reads=[gq], writes=[gm])
                _act(k, gm[:, :], gm[:, :], AF.Sqrt, [gm], [gm], scale=1.0 / H, bias=64e-5)
                k.op("dve", lambda e: e.reciprocal(out=gm[:, :], in_=gm[:, :]), reads=[gm], writes=[gm])
                _tt(k, gx[:, :, :], gx[:, :, :], gm[:, :].unsqueeze(2).to_broadcast([C, G, H]), ALU.mult, [gx, gm], [gx])
                _tt(k, gx[:, :, :], gx[:, :, :], lnxs[:, 0, j, :].unsqueeze(1).to_broadcast([C, G, H]), ALU.mult, [gx, lnxs], [gx])
                _tt(k, gx[:, :, :], gx[:, :, :], lnxs[:, 1, j, :].unsqueeze(1).to_broadcast([C, G, H]), ALU.add, [gx, lnxs], [gx])
                _tt(k, gx[:, :, :], gx[:, :, :], bon[j][:, :, :], ALU.add, [gx, bon[j]], [gx])
                _tt(k, gq[:, :, :], gx[:, :, :], gate[j][:, :, :], ALU.mult, [gx, gate[j]], [gq])
                k.dma("sp", yc.h[j, t0:t0 + TB, :].rearrange("(g p) d -> p g d", p=C), gq[:, :, :], reads=[gq], writes=[yc], semtile=gq, nowaw=True)


class M2(list):
    def __getitem__(self, i):
        return list.__getitem__(self, i % 2)


def prep_b1(inp, xT_b, hp_i):
    w = inp["od_w_in"][0]
    cols = []
    for j in range(4):
        h = 4 * hp_i + j
        for i in range(3):
            cols += list(range(i * 1024 + h * 64, i * 1024 + (h + 1) * 64))
    cols += list(range(3072, 3072 + 288))
    for hh in range(2):
        h = 2 * hp_i + hh
        for i in range(3):
            cols += list(range(3360 + i * 1024 + h * 128, 3360 + i * 1024 + (h + 1) * 128))
    cols += [6432 + 2 * hp_i, 6432 + 2 * hp_i + 1]
    wsel = np.ascontiguousarray(w[:, cols])
    mu_full = inp["rwkv_mu"][0]
    mus = np.zeros((128, 16), np.float32)
    for j in range(4):
        h = 4 * hp_i + j
        for i in range(3):
            mus[0:64, j * 3 + i] = mu_full[i * 1024 + h * 64:i * 1024 + (h + 1) * 64]
    mus[0:64, 12] = mu_full[3072:3136]
    mus[0:64, 13] = mu_full[3136:3200]
    mus[0:128, 14] = mu_full[3200:3328]
    mus[0:32, 15] = mu_full[3328:3360]
    hch = np.zeros((64, 4, 5), np.float32)
    for j in range(4):
        h = 4 * hp_i + j
        sl = slice(h * 64, (h + 1) * 64)
        hch[:, j, 0] = inp["rwkv_w0"][0][sl]
        hch[:, j, 1] = inp["rwkv_a0"][0][sl]
        hch[:, j, 2] = inp["rwkv_k_k"][0][sl]
        hch[:, j, 3] = inp["rwkv_k_a"][0][sl]
        hch[:, j, 4] = inp["rwkv_r_k"][0][h]
    csl = slice(hp_i * 256, (hp_i + 1) * 256)
    lnx = np.zeros((64, 2, 4, 64), np.float32)
    lnx[:, 0] = inp["rwkv_lnx_w"][0][csl].reshape(4, 64)[None]
    lnx[:, 1] = inp["rwkv_lnx_b"][0][csl].reshape(4, 64)[None]
    fbf = np.ascontiguousarray(np.broadcast_to(inp["fox_b_f"][0][2 * hp_i:2 * hp_i + 2][None, :], (128, 2))).astype(np.float32)
    return dict(xT=xT_b, wsel=wsel, mus=mus, hch=hch, w2=np.ascontiguousarray(inp["rwkv_w2"][0][:, csl]),
                a2=np.ascontiguousarray(inp["rwkv_a2"][0][:, csl]), g2=np.ascontiguousarray(inp["rwkv_g2"][0][:, csl]),
                lnx=lnx, fbf=fbf, cmat=_cmat())


def _b1_fox(k, L):
    sb, psb, cm, ident_b, ones_f, ones_b, W, proj, load_x, nextA = (L[n] for n in ("sb", "psb", "cm", "ident_b", "ones_f", "ones_b", "W", "proj", "load_x", "nextA"))
    NB, yd, xi, S = L["NB"], L["yd"], L["xi"], L["S"]
    NKB = S // 128
    fb = sb("fb", [128, 2], F32)
    k.dma("sp", fb[:, :], L["fbf"].h[:, :], reads=[L["fbf"]], writes=[fb])
    nfb = sb("nfb", [128, 2], F32)
    _ts(k, nfb[:, :], fb[:, :], -1.0, ALU.mult, [fb], [nfb])
    maskb = sb("maskb", [128, 128], BF16)
    _act(k, maskb[:, :], cm[:, 3, :], AF.Copy, [cm], [maskb])
    KT = sb("KT", [128, S], BF16)
    Vall = sb("Vall", [128, NKB, 129], BF16)
    k.op("dve", lambda e: e.memset(Vall[:, :, 128:129], 1.0), writes=[Vall])
    Call = sb("Call", [128, NKB], F32)
    biasK = sb("biasK", [128, NKB], F32)
    qTb = sb("qTb", [128, TB], BF16)
    sq = sb("sqf", [128, TB], BF16)
    lf = sb("lf", [128, 4], F32)
    cc = sb("cc", [128, 4], F32)
    crel = sb("crel", [128, 4], F32)
    carry = sb("carry", [128, 1], F32)
    cref = sb("cref", [128, 1], F32)
    kmax2 = sb("kmax2f", [128, 1], F32)
    kmt = sb("kmt", [128, 1], F32)
    mrow = sb("mrow", [1, TB], F32)
    shiftrow = sb("shiftrow", [1, TB], BF16)
    PT2 = [sb("PT0", [128, TB], BF16), sb("PT1", [128, TB], BF16)]
    rec = sb("rec", [128, 4], F32)
    yo = sb("yo", [128, 4, 128], F32)
    pC, pD = psb[2], psb[3]
    Oacc = [psb[4], psb[5], psb[6], psb[7]]
    npt = 0
    for hh in range(2):
        k.op("dve", lambda e: e.memset(carry[:, :], 0.0), writes=[carry])
        k.op("dve", lambda e: e.memset(kmax2[:, :], 0.0), writes=[kmax2])
        wb = hh * 384
        for blk in range(NB):
            t0 = blk * TB
            xb = load_x(blk, xi[0])
            xi[0] += 1
            ps = proj(xb, wb, 128)
            _act(k, qTb[:, :], ps[:, :], AF.Copy, [ps], [qTb], scale=float(128 ** -0.5))
            ps = proj(xb, wb + 128, 128)
            _act(k, KT[:, t0:t0 + TB], ps[:, :], AF.Copy, [ps], [KT])
            ps = nextA()
            for sbi in range(4):
                for kc in range(NCH):
                    k.mm((ps, ps[:, sbi * 128:(sbi + 1) * 128]), xb[:, kc, sbi * 128:(sbi + 1) * 128], W[:, kc, wb + 256:wb + 384], kc == 0, kc == NCH - 1, [xb, W])
            _act(k, Vall[:, blk * 4:blk * 4 + 4, 0:128], ps[:, :].rearrange("p (s d) -> p s d", d=128), AF.Copy, [ps], [Vall])
            for sbi in range(4):
                for kc in range(NCH):
                    k.mm((pC, pC[:, sbi:sbi + 1]), xb[:, kc, sbi * 128:(sbi + 1) * 128], W[:, kc, 768 + hh:769 + hh], kc == 0, kc == NCH - 1, [xb, W])
            _act(k, lf[:, :], pC[:, 0:4], AF.Exp, [pC, nfb], [lf], scale=-1.0, bias=nfb[:, hh:hh + 1])
            _act(k, lf[:, :], lf[:, :], AF.Ln, [lf], [lf], bias=1.0)
            _ts(k, lf[:, :], lf[:, :], -1.0, ALU.mult, [lf], [lf])
            k.mm((pC, pC[:, 0:4]), cm[:, 3, :], lf[:, :], True, True, [cm, lf])
            k.mm((pD, pD[:, 0:4]), ones_f[:, :], lf[:, :], True, True, [ones_f, lf])
            k.op("dve", lambda e: e.tensor_copy(out=cref[:, :], in_=carry[:, :]), reads=[carry], writes=[cref])
            for sbi in range(4):
                _ts(k, cc[:, sbi:sbi + 1], pC[:, sbi:sbi + 1], carry[:, 0:1], ALU.add, [pC, carry], [cc], nowaw=sbi > 0)
                _tt(k, carry[:, :], carry[:, :], pD[:, sbi:sbi + 1], ALU.add, [carry, pD], [carry])
            k.op("dve", lambda e: e.tensor_copy(out=Call[:, blk * 4:blk * 4 + 4], in_=cc[:, :]), reads=[cc], writes=[Call])
            _ts(k, crel[:, :], cc[:, :], cref[:, 0:1], ALU.subtract, [cc, cref], [crel])
            nkb = (blk + 1) * 4
            _ts(k, biasK[:, 0:nkb], Call[:, 0:nkb], -1.0, ALU.mult, [Call, cref], [biasK], s2=cref[:, 0:1], op1=ALU.add)
            _tt(k, sq[:, :], KT[:, t0:t0 + TB], KT[:, t0:t0 + TB], ALU.mult, [KT], [sq])
            ps = nextA()
            k.mm(ps, ones_b[:, :], sq[:, :], True, True, [ones_b, sq])
            k.op("dve", lambda e: e.tensor_reduce(out=kmt[:, :], in_=ps[:, :], axis=AX.X, op=ALU.max), reads=[ps], writes=[kmt])
            _tt(k, kmax2[:, :], kmax2[:, :], kmt[:, :], ALU.max, [kmax2, kmt], [kmax2])
            _tt(k, sq[:, :], qTb[:, :], qTb[:, :], ALU.mult, [qTb], [sq])
            k.mm((pD, pD[0:1, :]), ones_b[:, 0:1], sq[:, :], True, True, [ones_b, sq])
            _act(k, mrow[0:1, :], pD[0:1, :], AF.Sqrt, [pD, kmax2], [mrow], scale=kmax2[0:1, 0:1])
            for sbi in range(4):
                k.mm((pC, pC[0:1, sbi * 128:(sbi + 1) * 128]), crel[:, sbi:sbi + 1], cm[:, 0, :], True, True, [crel, cm])
            _stt(k, shiftrow[0:1, :], mrow[0:1, :], -1.1, pC[0:1, :], ALU.mult, ALU.add, [mrow, pC], [shiftrow])
            for j in range(nkb):
                qlo = max(0, j - blk * 4)
                c0 = qlo * 128
                ps = nextA()
                k.mm((ps, ps[:, c0:TB]), KT[:, j * 128:(j + 1) * 128], qTb[:, c0:TB], True, False, [KT, qTb])
                k.mm((ps, ps[:, c0:TB]), ones_b[0:1, :], shiftrow[0:1, c0:TB], False, True, [ones_b, shiftrow])
                PT = PT2[npt % 2]
                npt += 1
                _act(k, PT[:, c0:TB], ps[:, c0:TB], AF.Exp, [ps, biasK], [PT], bias=biasK[:, j:j + 1])
                if j >= blk * 4:
                    _tt(k, PT[:, c0:c0 + 128], PT[:, c0:c0 + 128], maskb[:, :], ALU.mult, [PT, maskb], [PT])
                for sbq in range(qlo, 4):
                    acc = Oacc[sbq]
                    k.mm((acc, acc[:, 0:129]), PT[:, sbq * 128:(sbq + 1) * 128], Vall[:, j, :], j == 0, j == blk * 4 + sbq, [PT, Vall])
            for sbq in range(4):
                acc = Oacc[sbq]
                k.op("dve", lambda e: e.reciprocal(out=rec[:, sbq:sbq + 1], in_=acc[:, 128:129]), reads=[acc], writes=[rec], nowaw=True)
                _ts(k, yo[:, sbq, :], acc[:, 0:128], rec[:, sbq:sbq + 1], ALU.mult, [acc, rec], [yo], nowaw=sbq > 0)
            k.dma("sp", yd.h[hh, t0:t0 + TB, :].rearrange("(s p) d -> p s d", p=128), yo[:, :, :], reads=[yo], writes=[yd], semtile=yo, nowaw=True)


_PROG = {}


def _stage_c_maps(inp, i, w_out, xT, yT, S):
    def pk(v):
        return np.ascontiguousarray(v.reshape(16, 128).T)
    lnp = np.stack([pk(inp["ln_mix_w"][i]), pk(inp["ln_mix_b"][i]), pk(inp["ln_xa_w"][i]), pk(inp["ln_xa_b"][i]),
                    pk(inp["ln_ffn_w"][i]), pk(inp["ln_ffn_b"][i])], 1).astype(np.float32)
    convw = np.ascontiguousarray(inp["ffn_conv_w"][i].T.reshape(88, 128, 3).transpose(1, 0, 2)).astype(np.float32)
    per = S // 4
    maps = []
    for c in range(NCORES):
        b, q = c // 4, c % 4
        t0 = q * per
        xs = np.zeros((D, HALO + per), np.float32)
        ys = np.zeros((D, HALO + per), np.float32)
        xs[:, HALO:] = xT[b][:, t0:t0 + per]
        ys[:, HALO:] = yT[b][:, t0:t0 + per]
        if q > 0:
            xs[:, :HALO] = xT[b][:, t0 - HALO:t0]
            ys[:, :HALO] = yT[b][:, t0 - HALO:t0]
        hm = np.full((128, 1), 0.0 if q == 0 else 1.0, np.float32)
        maps.append(dict(xT=xs, yT=ys, memT=np.ascontiguousarray(inp["mem"][b].T), w_out=w_out, w_q=inp["xa_w_q"][i],
                         w_kv=inp["xa_w_kv"][i], w_o=inp["xa_w_o"][i], w_up=inp["ffn_w_up"][i], w_down=inp["ffn_w_down"][i],
                         lnp=lnp, convw=convw, halomask=hm))
    return maps


def kernel(**inp):
    inp = {k_: np.asarray(v) for k_, v in inp.items()}
    x = inp["x"]
    B, S, _ = x.shape
    per = S // 4
    cores = list(range(NCORES))
    xT = [np.ascontiguousarray(x[b].T) for b in range(B)]
    kb0 = build_b0(S)
    res = run_bass_kernel_spmd(kb0.nc, [prep_b0(inp, xT[c // 4], c % 4) for c in cores], core_ids=cores).results
    yT = [np.zeros((D, S), np.float32) for _ in range(B)]
    for c in cores:
        b, g = c // 4, c % 4
        yT[b][g * 256:(g + 1) * 256, :] = res[c]["ya"]
        for hh in range(2):
            h = 2 * g + hh
            yT[b][1024 + h * 128:1024 + (h + 1) * 128, :] = res[c]["yb"][hh].T
    del res
    kc = build_stage_c(per // 512)
    res = run_bass_kernel_spmd(kc.nc, _stage_c_maps(inp, 0, inp["ev_w_out"][0], xT, yT, S), core_ids=cores).results
    for c in cores:
        b, q = c // 4, c % 4
        xT[b][:, q * per:(q + 1) * per] = res[c]["outT"]
    del res
    kb1 = build_b1(S)
    res = run_bass_kernel_spmd(kb1.nc, [prep_b1(inp, xT[c // 4], c % 4) for c in cores], core_ids=cores).results
    for c in cores:
        b, g = c // 4, c % 4
        for j in range(4):
            h = 4 * g + j
            yT[b][h * 64:(h + 1) * 64, :] = res[c]["yc"][j].T
        for hh in range(2):
            h = 2 * g + hh
            yT[b][1024 + h * 128:1024 + (h + 1) * 128, :] = res[c]["yd"][hh].T
    del res
    kc = build_stage_c(per // 512)
    res = run_bass_kernel_spmd(kc.nc, _stage_c_maps(inp, 1, inp["od_w_out"][0], xT, yT, S), core_ids=cores).results
    out = np.zeros((B, S, D), np.float32)
    for c in cores:
        b, q = c // 4, c % 4
        out[b, q * per:(q + 1) * per, :] = res[c]["outT"].T
    return out
```
